# Optimizing a Trainium2 kernel written in Bass

```python
import jax, jax.numpy as jnp
from jax import lax
import numpy as np

D_MODEL = 1024
BATCH = 16
SEQ = 256
DEPTH = 2
DEC_BATCH = 4
DEC_SEQ = 2048
PAST_LEN = 256

GRID_W = 64
N_HEADS = 8
KV_HEADS = 2
HEAD_DIM = 64
Q_GROUP = N_HEADS // KV_HEADS
ATTN_WIDTH = N_HEADS * HEAD_DIM
KV_WIDTH = KV_HEADS * HEAD_DIM
WINDOW = 128
BLOCK = 128
AXIS_ROT = HEAD_DIM // 2
ROPE_BASE = 10000.0
LRU_WIDTH = 512
LRU_BLOCKS = 8
LRU_BW = LRU_WIDTH // LRU_BLOCKS
LRU_C = 8.0
CONV_W = 4
CONV_LEFT = 2
FOURIER_WIDTH = 512
FOURIER_GROUPS = 4
FOURIER_GW = FOURIER_WIDTH // FOURIER_GROUPS
D_FF = 2816
N_BRANCH = 3
N_SUB = 3
EPS = 1e-6
NEG = -1e30
IN_WIDTH = ATTN_WIDTH + 2 * KV_WIDTH + 2 * LRU_WIDTH + FOURIER_WIDTH
SPLITS = [ATTN_WIDTH,
          ATTN_WIDTH + KV_WIDTH,
          ATTN_WIDTH + 2 * KV_WIDTH,
          ATTN_WIDTH + 2 * KV_WIDTH + LRU_WIDTH,
          ATTN_WIDTH + 2 * KV_WIDTH + 2 * LRU_WIDTH]

kernel_name = "hybrid_diffusion_prefix_step"

f32 = jnp.float32


def rmsnorm(x, g):
    xf = x.astype(f32)
    y = xf * lax.rsqrt(jnp.mean(xf * xf, axis=-1, keepdims=True) + EPS)
    return (y * g.astype(f32)).astype(x.dtype)


def swiglu(h, wg, wu, wd):
    return (jax.nn.silu(h @ wg) * (h @ wu)) @ wd


def axial_rope_tables(n_tokens):
    rows = n_tokens // GRID_W
    row = jnp.repeat(jnp.arange(rows), GRID_W).astype(f32)
    col = jnp.tile(jnp.arange(GRID_W), rows).astype(f32)
    inv = ROPE_BASE ** (-jnp.arange(0, AXIS_ROT, 2, dtype=f32) / AXIS_ROT)
    ang = jnp.stack([row[:, None] * inv, col[:, None] * inv], axis=1)
    return jnp.cos(ang), jnp.sin(ang)


def apply_rope(x, cos, sin):
    B, L, H, _ = x.shape
    xr = x.astype(f32).reshape(B, L, H, 2, 2, AXIS_ROT // 2)
    x1, x2 = xr[..., 0, :], xr[..., 1, :]
    c, s = cos[:, None], sin[:, None]
    out = jnp.stack([x1 * c - x2 * s, x1 * s + x2 * c], axis=-2)
    return out.reshape(B, L, H, HEAD_DIM).astype(x.dtype)


def sink_attend(q, k, v, bias, sink):
    s = jnp.einsum('bqhgd,bkhd->bhgqk', q.astype(f32), k.astype(f32)) * (HEAD_DIM ** -0.5)
    if bias is not None:
        s = s + bias
    sk = sink.astype(f32)[None, :, :, None, None]
    m = jnp.maximum(jnp.max(s, axis=-1, keepdims=True), sk)
    p = jnp.exp(s - m)
    denom = jnp.sum(p, axis=-1, keepdims=True) + jnp.exp(sk - m)
    o = jnp.einsum('bhgqk,bkhd->bqhgd', p / denom, v.astype(f32))
    return o.astype(q.dtype)


def context_attention(q, k, v, sink):
    B, L = q.shape[:2]
    nb = L // BLOCK
    qb = q.reshape(B, nb, BLOCK, KV_HEADS, Q_GROUP, HEAD_DIM).swapaxes(0, 1)
    ob = lax.map(lambda qi: sink_attend(qi, k, v, None, sink), qb)
    return ob.swapaxes(0, 1).reshape(B, L, ATTN_WIDTH)


def latent_attention(q, k, v, k_ctx, v_ctx, sink):
    B, L = q.shape[:2]
    nb = L // BLOCK
    pad = ((0, 0), (BLOCK, BLOCK), (0, 0), (0, 0))
    kp, vp = jnp.pad(k, pad), jnp.pad(v, pad)
    qb = q.reshape(B, nb, BLOCK, KV_HEADS, Q_GROUP, HEAD_DIM).swapaxes(0, 1)
    a_idx = jnp.arange(BLOCK)[:, None]
    b_idx = jnp.arange(3 * BLOCK)[None, :]
    band_ok = jnp.abs(b_idx - BLOCK - a_idx) <= WINDOW
    ctx_bias = jnp.zeros((BLOCK, k_ctx.shape[1]), f32)

    def one_block(args):
        qi, n = args
        start = n * BLOCK
        kb = lax.dynamic_slice_in_dim(kp, start, 3 * BLOCK, axis=1)
        vb = lax.dynamic_slice_in_dim(vp, start, 3 * BLOCK, axis=1)
        ppos = start + b_idx
        ok = band_ok & (ppos >= BLOCK) & (ppos < L + BLOCK)
        bias = jnp.concatenate([jnp.where(ok, 0.0, NEG).astype(f32), ctx_bias], axis=1)
        keys = jnp.concatenate([kb, k_ctx.astype(kb.dtype)], axis=1)
        vals = jnp.concatenate([vb, v_ctx.astype(vb.dtype)], axis=1)
        return sink_attend(qi, keys, vals, bias, sink)

    ob = lax.map(one_block, (qb, jnp.arange(nb)))
    return ob.swapaxes(0, 1).reshape(B, L, ATTN_WIDTH)


def short_conv(x, w, bias):
    L = x.shape[1]
    xp = jnp.pad(x, ((0, 0), (CONV_LEFT, CONV_W - 1 - CONV_LEFT), (0, 0)))
    out = bias
    for j in range(CONV_W):
        out = out + xp[:, j:j + L] * w[j]
    return out


def block_diag(x, w):
    B, L, _ = x.shape
    y = jnp.einsum('blnc,ncd->blnd', x.reshape(B, L, LRU_BLOCKS, LRU_BW), w.astype(f32))
    return y.reshape(B, L, LRU_WIDTH)


def rglru_coeffs(x, wa, ba, wi, bi, lam):
    xf = x.astype(f32)
    r = jax.nn.sigmoid(block_diag(xf, wa) + ba.astype(f32))
    i = jax.nn.sigmoid(block_diag(xf, wi) + bi.astype(f32))
    log_a = -LRU_C * r * jax.nn.softplus(-lam.astype(f32))
    a = jnp.exp(log_a)
    b = jnp.sqrt(-jnp.expm1(2.0 * log_a)) * (i * xf)
    return a, b


def linear_scan(a, b, h0, reverse):
    edge = -1 if reverse else 0
    b = b.at[:, edge].add(a[:, edge] * h0)

    def combine(e1, e2):
        a1, b1 = e1
        a2, b2 = e2
        return a1 * a2, a2 * b1 + b2

    _, h = lax.associative_scan(combine, (a, b), reverse=reverse, axis=1)
    return h


def bidir_rglru(x, wa, ba, wi, bi, lam, h0):
    af, bf = rglru_coeffs(x, wa[0], ba[0], wi[0], bi[0], lam[0])
    h_fwd = linear_scan(af, bf, h0[:, 0].astype(f32), False)
    ab, bb = rglru_coeffs(x, wa[1], ba[1], wi[1], bi[1], lam[1])
    h_bwd = linear_scan(ab, bb, h0[:, 1].astype(f32), True)
    return h_fwd, h_bwd


def fourier_mix(x):
    B, L, _ = x.shape
    xf = x.astype(f32).reshape(B, L, FOURIER_GROUPS, FOURIER_GW)
    y = jnp.fft.fft2(xf, axes=(1, 3), norm="ortho").real
    return y.reshape(B, L, FOURIER_WIDTH).astype(x.dtype)


def token_mixer(h, P, rope, ctx):
    B, L, _ = h.shape
    q, k, v, xr, yr, xf = jnp.split(h @ P["w_in"], SPLITS, axis=-1)
    q = q.reshape(B, L, N_HEADS, HEAD_DIM)
    k = k.reshape(B, L, KV_HEADS, HEAD_DIM)
    v = v.reshape(B, L, KV_HEADS, HEAD_DIM)
    sink = P["attn_sink"].reshape(KV_HEADS, Q_GROUP)
    xc = short_conv(xr, P["conv_w"], P["conv_b"])
    if ctx is None:
        attn = context_attention(q, k, v, sink)
        h0 = jnp.zeros((B, 2, LRU_WIDTH), f32)
    else:
        k_ctx, v_ctx, h0 = ctx
        cos, sin = rope
        q = apply_rope(q, cos, sin)
        k = apply_rope(k, cos, sin)
        attn = latent_attention(q, k, v, k_ctx, v_ctx, sink)
    h_fwd, h_bwd = bidir_rglru(xc, P["lru_wa"], P["lru_ba"], P["lru_wi"], P["lru_bi"], P["lru_lambda"], h0)
    rec = ((h_fwd + h_bwd) * jax.nn.gelu(yr.astype(f32))).astype(h.dtype)
    four = fourier_mix(xf)
    g = jax.nn.sigmoid(h @ P["w_branch_gate"] + P["b_branch_gate"])
    ga, gr, gf = jnp.split(g, N_BRANCH, axis=-1)
    merged = (ga * (attn @ P["w_attn_out"]) + gr * (rec @ P["w_lru_out"])
              + gf * (four @ P["w_fourier_out"]))
    out = merged @ P["w_o"]
    if ctx is None:
        return out, (k, v, jnp.stack([h_fwd[:, -1], h_bwd[:, 0]], axis=1))
    return out, None


def trunk_layer(x, cond, P, rope, ctx):
    M = cond.shape[0]
    mod = (jax.nn.silu(cond) @ P["w_ada"] + P["b_ada"]).reshape(M, 1, N_SUB, 3, D_MODEL)

    def modnorm(z, i):
        return rmsnorm(z, P["norm_w"][i]) * (1.0 + mod[:, :, i, 1]) + mod[:, :, i, 0]

    x = x + 0.5 * mod[:, :, 0, 2] * swiglu(modnorm(x, 0), P["ffn1_wg"], P["ffn1_wu"], P["ffn1_wd"])
    mix, ctx_out = token_mixer(modnorm(x, 1), P, rope, ctx)
    x = x + mod[:, :, 1, 2] * mix
    x = x + 0.5 * mod[:, :, 2, 2] * swiglu(modnorm(x, 2), P["ffn2_wg"], P["ffn2_wu"], P["ffn2_wd"])
    return x, ctx_out


def setup_inputs(seed: int = 0) -> dict:
    key = jax.random.key(seed)
    keys = iter(jax.random.split(key, 48))

    def nrm(shape, scale):
        return jax.random.normal(next(keys), shape, f32) * scale

    D = D_MODEL
    lo, hi = 0.9 ** (1.0 / LRU_C), 0.999 ** (1.0 / LRU_C)
    a0 = jax.random.uniform(next(keys), (DEPTH, 2, LRU_WIDTH), f32, minval=lo, maxval=hi)
    return {
        "x_prompt": nrm((BATCH, SEQ, D), 1.0),
        "x_sample": nrm((DEC_BATCH, DEC_SEQ, D), 1.0),
        "c": nrm((DEC_BATCH, D), 1.0),
        "cache_k": nrm((DEC_BATCH, DEPTH, PAST_LEN, KV_HEADS, HEAD_DIM), 1.0),
        "cache_v": nrm((DEC_BATCH, DEPTH, PAST_LEN, KV_HEADS, HEAD_DIM), 1.0),
        "state_lru": nrm((DEC_BATCH, DEPTH, 2, LRU_WIDTH), 0.5),
        "c_ctx": nrm((D,), 1.0),
        "w_ada": nrm((DEPTH, D, N_SUB * 3 * D), 0.5 * D ** -0.5),
        "b_ada": nrm((DEPTH, N_SUB * 3 * D), 0.01),
        "norm_w": 1.0 + nrm((DEPTH, N_SUB, D), 0.01),
        "final_norm_w": 1.0 + nrm((D,), 0.01),
        "ffn1_wg": nrm((DEPTH, D, D_FF), D ** -0.5),
        "ffn1_wu": nrm((DEPTH, D, D_FF), D ** -0.5),
        "ffn1_wd": nrm((DEPTH, D_FF, D), D_FF ** -0.5),
        "ffn2_wg": nrm((DEPTH, D, D_FF), D ** -0.5),
        "ffn2_wu": nrm((DEPTH, D, D_FF), D ** -0.5),
        "ffn2_wd": nrm((DEPTH, D_FF, D), D_FF ** -0.5),
        "w_in": nrm((DEPTH, D, IN_WIDTH), D ** -0.5),
        "w_branch_gate": nrm((DEPTH, D, N_BRANCH * D), D ** -0.5),
        "b_branch_gate": nrm((DEPTH, N_BRANCH * D), 0.01),
        "attn_sink": nrm((DEPTH, N_HEADS), 0.5),
        "w_attn_out": nrm((DEPTH, ATTN_WIDTH, D), ATTN_WIDTH ** -0.5),
        "conv_w": nrm((DEPTH, CONV_W, LRU_WIDTH), CONV_W ** -0.5),
        "conv_b": nrm((DEPTH, LRU_WIDTH), 0.01),
        "lru_wa": nrm((DEPTH, 2, LRU_BLOCKS, LRU_BW, LRU_BW), LRU_BW ** -0.5),
        "lru_ba": nrm((DEPTH, 2, LRU_WIDTH), 0.01),
        "lru_wi": nrm((DEPTH, 2, LRU_BLOCKS, LRU_BW, LRU_BW), LRU_BW ** -0.5),
        "lru_bi": nrm((DEPTH, 2, LRU_WIDTH), 0.01),
        "lru_lambda": jnp.log(a0) - jnp.log1p(-a0),
        "w_lru_out": nrm((DEPTH, LRU_WIDTH, D), LRU_WIDTH ** -0.5),
        "w_fourier_out": nrm((DEPTH, FOURIER_WIDTH, D), FOURIER_WIDTH ** -0.5),
        "w_o": nrm((DEPTH, D, D), D ** -0.5),
    }


def reference(x_prompt, x_sample, c, cache_k, cache_v, state_lru, c_ctx,
              w_ada, b_ada, norm_w, final_norm_w,
              ffn1_wg, ffn1_wu, ffn1_wd, ffn2_wg, ffn2_wu, ffn2_wd,
              w_in, w_branch_gate, b_branch_gate, attn_sink, w_attn_out,
              conv_w, conv_b, lru_wa, lru_ba, lru_wi, lru_bi, lru_lambda,
              w_lru_out, w_fourier_out, w_o):
    rope = axial_rope_tables(x_sample.shape[1])
    cond_ctx = c_ctx[None, :]
    xp, xs = x_prompt, x_sample
    ks, vs, ss = [], [], []
    for l in range(DEPTH):
        P = dict(w_ada=w_ada[l], b_ada=b_ada[l], norm_w=norm_w[l],
                 ffn1_wg=ffn1_wg[l], ffn1_wu=ffn1_wu[l], ffn1_wd=ffn1_wd[l],
                 ffn2_wg=ffn2_wg[l], ffn2_wu=ffn2_wu[l], ffn2_wd=ffn2_wd[l],
                 w_in=w_in[l], w_branch_gate=w_branch_gate[l], b_branch_gate=b_branch_gate[l],
                 attn_sink=attn_sink[l], w_attn_out=w_attn_out[l],
                 conv_w=conv_w[l], conv_b=conv_b[l],
                 lru_wa=lru_wa[l], lru_ba=lru_ba[l], lru_wi=lru_wi[l], lru_bi=lru_bi[l],
                 lru_lambda=lru_lambda[l], w_lru_out=w_lru_out[l],
                 w_fourier_out=w_fourier_out[l], w_o=w_o[l])
        xp, (k_l, v_l, s_l) = trunk_layer(xp, cond_ctx, P, None, None)
        ks.append(k_l)
        vs.append(v_l)
        ss.append(s_l)
        xs, _ = trunk_layer(xs, c, P, rope, (cache_k[:, l], cache_v[:, l], state_lru[:, l]))
    y_prompt = rmsnorm(xp, final_norm_w)
    y_sample = rmsnorm(xs, final_norm_w)
    new_cache_k = jnp.stack(ks, axis=1)
    new_cache_v = jnp.stack(vs, axis=1)
    new_state_lru = jnp.stack(ss, axis=1)
    return (y_prompt, y_sample, new_cache_k, new_cache_v, new_state_lru)
```

```python
import os
import numpy as np
import ml_dtypes
import concourse.bass as bass
import concourse.mybir as mybir
from concourse.bass_utils import run_bass_kernel_spmd
from concourse.ap import AP

F32 = mybir.dt.float32
BF16 = mybir.dt.bfloat16
AF = mybir.ActivationFunctionType
ALU = mybir.AluOpType
BF = ml_dtypes.bfloat16

D = 1024
KC = 8
DFF = 2816
JC = 22
DEPTH = 2
NTOK = 1536
TT = 512
NTILE = 3
NCOLIN = 2944
PAYF = 10240
EPS = 1e-6
MASKV = -30000.0
RING_SLOTS = 3
SLOT_ELEMS = 6144

SP_OFF = {}
_o = 0
for _n, _w in [("cond", 16), ("bada", DEPTH * 144), ("normw", DEPTH * 24), ("fnw", 8), ("bgate", DEPTH * 24),
               ("convw", DEPTH * 16), ("convb", DEPTH * 4), ("lba", DEPTH * 8), ("lbi", DEPTH * 8),
               ("lam", DEPTH * 8), ("sink", DEPTH * 8), ("h0", DEPTH * 8), ("sel", 2)]:
    SP_OFF[_n] = (_o, _w)
    _o += _w
NS = _o


def bcast_last(ap, n):
    return AP(ap.tensor, ap.offset, [list(d) for d in ap.ap] + [[0, n]])


def bcast_mid(ap, n):
    a = [list(d) for d in ap.ap]
    return AP(ap.tensor, ap.offset, [a[0], [0, n]] + a[1:])


def rev_ap(ap):
    a = [list(d) for d in ap.ap]
    st, cnt = a[-1]
    return AP(ap.tensor, ap.offset + st * (cnt - 1), a[:-1] + [[-st, cnt]])


class Ev:
    __slots__ = ("sem", "val")

    def __init__(self, sem, val):
        self.sem = sem
        self.val = val


class Eng:
    def __init__(self, name, obj, sem):
        self.name = name
        self.obj = obj
        self.sem = sem
        self.cnt = 0
        self.seen = {}


class B:
    def __init__(self, nc):
        self.nc = nc
        self.lastw = {}
        self.readers = {}
        self.dsems = {}
        self.engs = {}
        for name, obj in [("pe", nc.tensor), ("act", nc.scalar), ("dve", nc.vector), ("pool", nc.gpsimd),
                          ("sp", nc.sync)]:
            sem = nc.semaphore("s_" + name).__enter__()
            self.engs[name] = Eng(name, obj, sem)
        self.out_evs = []

    def wait(self, eng, ev):
        k = id(ev.sem)
        if eng.seen.get(k, 0) >= ev.val:
            return
        eng.obj.wait_ge(ev.sem, ev.val)
        eng.seen[k] = ev.val

    def _deps(self, reads, writes):
        evs = []
        for k in reads:
            w = self.lastw.get(k)
            if w is not None:
                evs.append(w)
        for k in writes:
            w = self.lastw.get(k)
            if w is not None:
                evs.append(w)
            evs.extend(self.readers.get(k, ()))
        return evs

    def _commit(self, ev, reads, writes):
        for k in reads:
            self.readers.setdefault(k, []).append(ev)
        for k in writes:
            self.lastw[k] = ev
            self.readers[k] = []

    def op(self, en, fn, reads=(), writes=(), mark=True):
        eng = self.engs[en]
        for ev in self._deps(reads, writes):
            self.wait(eng, ev)
        ins = fn(eng.obj)
        if not mark:
            return None
        eng.cnt += 1
        ins.then_inc(eng.sem, 1)
        ev = Ev(eng.sem, eng.cnt)
        self._commit(ev, reads, writes)
        return ev

    def dma(self, en, out, in_, reads=(), writes=(), skey=None, is_out=False):
        eng = self.engs[en]
        for ev in self._deps(reads, writes):
            self.wait(eng, ev)
        if skey not in self.dsems:
            self.dsems[skey] = [self.nc.semaphore("d_%d" % len(self.dsems)).__enter__(), 0]
        ent = self.dsems[skey]
        ent[1] += 16
        eng.obj.dma_start(out=out, in_=in_).then_inc(ent[0], 16)
        ev = Ev(ent[0], ent[1])
        self._commit(ev, reads, writes)
        if is_out:
            self.out_evs.append(ev)
        return ev

    def barrier(self, evs, engines=("pe", "act", "dve", "pool", "sp")):
        for en in engines:
            for ev in evs:
                self.wait(self.engs[en], ev)


def build():
    nc = bass.Bass("TRN2", target_bir_lowering=False)

    def din(name, shape, dt=F32):
        return nc.dram_tensor(name, list(shape), dt, kind="ExternalInput").ap()

    def dout(name, shape, dt=F32):
        return nc.dram_tensor(name, list(shape), dt, kind="ExternalOutput").ap()

    xin = din("xin", [128, KC, NTOK])
    smallp_d = din("smallp", [128, NS])
    ident_d = din("ident", [128, 128], BF16)
    ones_d = din("ones", [128, 128], BF16)
    masks_d = din("masks", [128, 4, 128], BF16)
    rope_d = din("rope", [128, 2, 1024], BF16)
    ctxk_d = din("ctxk", [128, DEPTH, 256])
    ctxv_d = din("ctxv", [128, DEPTH, 2, 128])
    lruw_d = din("lruw", [128, DEPTH * 16, 128])
    dftc_d = din("dftc", [128, 256], BF16)
    dftp_d = din("dftp", [128, 2, 2, 256], BF16)
    dfts_d = din("dfts", [4, 128, 16, 512], BF16)
    w_ada = din("w_ada", [DEPTH, D, 9216])
    ffn_w = {}
    for i in (1, 2):
        ffn_w[i] = (din("ffn%d_wg" % i, [DEPTH, D, DFF]), din("ffn%d_wu" % i, [DEPTH, D, DFF]),
                    din("ffn%d_wd" % i, [DEPTH, DFF, D]))
    w_in = din("w_in", [DEPTH, D, NCOLIN])
    w_gate = din("w_gate", [DEPTH, 8, 128, KC, 384])
    w_out = din("w_out", [DEPTH, 8, 128, 12, 128])
    w_o = din("w_o", [DEPTH, D, D])

    y_d = dout("y", [128, KC, NTOK])
    kout_d = dout("kout", [DEPTH, 128, 512])
    vout_d = dout("vout", [DEPTH, 128, 4, 128])
    sout_d = dout("sout", [128, DEPTH * 16])

    payA_t = nc.dram_tensor("payA", [128, 6144], BF16)
    gatA_t = nc.dram_tensor("gatA", [256, 6144], BF16)
    payB_t = nc.dram_tensor("payB", [128, 4096], BF16)
    gatB_t = nc.dram_tensor("gatB", [256, 4096], BF16)
    gat_d = gatA_t.ap()
    gatB_d = gatB_t.ap()

    bld = B(nc)
    op, dma = bld.op, bld.dma
    STOP = os.environ.get("KSTOP", "")

    def sb(name, shape, dt):
        return nc.sbuf_tensor(name, list(shape), dt).__enter__()

    x = sb("x", [128, KC, NTOK], F32)
    ring = sb("ring", [128, RING_SLOTS * SLOT_ELEMS], BF16)
    smallp = sb("smallp_sb", [128, NS], F32)
    ident = sb("ident_sb", [128, 128], BF16)
    ones = sb("ones_sb", [128, 128], BF16)
    masks = sb("masks_sb", [128, 4, 128], BF16)
    rope = sb("rope_sb", [128, 2, 1024], BF16)
    ctxk = sb("ctxk_sb", [128, DEPTH, 256], BF16)
    ctxv = sb("ctxv_sb", [128, DEPTH, 2, 2, 65], BF16)
    lruw = sb("lruw_sb", [128, 16, 128], BF16)
    dftc = sb("dftc_sb", [128, 256], BF16)
    dftp = sb("dftp_sb", [128, 2, 2, 256], BF16)
    condbf = sb("condbf", [128, KC, 2], BF16)
    modT = sb("modT", [128, DEPTH, 72, 2], F32)
    modA = sb("modA", [128, DEPTH, 3, KC, 2], F32)
    modG = sb("modG", [128, DEPTH, 3, KC, 2], F32)
    hclam = sb("hclam", [128, DEPTH * 8], F32)
    esink = sb("esink", [128, DEPTH * 8], F32)
    hba = sb("hba", [128, DEPTH * 8], F32)
    hbi = sb("hbi", [128, DEPTH * 8], F32)
    hbg = sb("hbg", [128, DEPTH * 24], F32)
    sout = sb("sout_sb", [128, DEPTH * 16], F32)
    ARENA_ELEMS = 52 * 1024
    print("sbuf bytes remaining before arena:", nc.sbuf_bytes_remaining)
    arena = sb("arena", [128, ARENA_ELEMS], BF16)

    def carve(off_bytes, shape, dt):
        n = 1
        for s_ in shape[1:]:
            n *= s_
        esz = 4 if dt == F32 else 2
        assert off_bytes % 4 == 0
        assert off_bytes + n * esz <= ARENA_ELEMS * 2, (off_bytes, shape)
        v = arena[:, off_bytes // 2: off_bytes // 2 + n * esz // 2]
        if dt == F32:
            v = v.bitcast(F32)
        if len(shape) == 2:
            return v
        names = " ".join("d%d" % i for i in range(len(shape) - 1))
        kw = {"d%d" % i: shape[i + 1] for i in range(len(shape) - 2)}
        return v.rearrange("p (%s) -> p %s" % (names, names), **kw)

    KB = 1024

    def spv(name, a=0, b=None):
        o, w = SP_OFF[name]
        if b is None:
            b = w
        return smallp[:, o + a: o + b]

    ps = [nc.psum_tensor("ps%d" % i, [128, 512], F32).__enter__() for i in range(8)]

    def PS(i):
        return ("ps", i)

    wstate = {"n": 0}

    def wpiece(parts):
        slot = wstate["n"] % RING_SLOTS
        wstate["n"] += 1
        views = []
        for (eo, shape, src) in parts:
            n = 1
            for s_ in shape[1:]:
                n *= s_
            assert eo + n <= SLOT_ELEMS
            v = ring[:, slot * SLOT_ELEMS + eo: slot * SLOT_ELEMS + eo + n]
            if len(shape) == 3:
                v = v.rearrange("p (a b) -> p a b", a=shape[1])
            dma("pool", v, src, reads=(), writes=[("w", slot)], skey=("w", slot))
            views.append(v)
        return slot, views

    def wsrc_rows(w2d, c0, c1):
        return w2d[:, c0:c1].rearrange("(k p) c -> p k c", p=128)

    def mm_group(out_ap, pairs, reads, writes, first_start=True, last_stop=True):
        n = len(pairs)
        ev = None
        for idx, (l_, r_) in enumerate(pairs):
            st = first_start and idx == 0
            sp_ = last_stop and idx == n - 1
            if idx == 0 or idx == n - 1:
                ev = op("pe", lambda e, l_=l_, r_=r_, st=st, sp_=sp_: e.matmul(out_ap, lhsT=l_, rhs=r_, start=st,
                                                                             stop=sp_),
                        reads=reads, writes=writes)
            else:
                op("pe", lambda e, l_=l_, r_=r_: e.matmul(out_ap, lhsT=l_, rhs=r_, start=False, stop=False),
                   mark=False)
        return ev

    evs = []
    evs.append(dma("sp", smallp[:], smallp_d[:, :], writes=["smallp"], skey="c0"))
    evs.append(dma("sp", ident[:], ident_d[:, :], skey="c1"))
    evs.append(dma("sp", ones[:], ones_d[:, :], skey="c2"))
    evs.append(dma("sp", masks[:], masks_d[:, :, :], skey="c3"))
    evs.append(dma("sp", rope[:], rope_d[:, :, :], skey="c4"))
    evs.append(dma("sp", dftc[:], dftc_d[:, :], skey="c5"))
    evs.append(dma("sp", dftp[:], dftp_d[:, :, :, :], skey="c6"))
    evs.append(dma("sp", x[:], xin[:, :, :], writes=[("x", 0), ("x", 1), ("x", 2)], skey="c7"))
    evs.append(op("dve", lambda e: e.memset(ctxv[:], 1.0)))
    bld.barrier(evs[-1:], engines=("pool",))
    evs.append(dma("pool", ctxk[:], ctxk_d[:, :, :], skey="c8"))
    for l in range(DEPTH):
        for blk in range(2):
            evs.append(dma("pool", ctxv[:, l, blk, :, 0:64],
                           ctxv_d[:, l, blk, :].rearrange("p (g d) -> p g d", g=2), skey="c9%d%d" % (l, blk)))
    bld.barrier(evs)

    op("act", lambda e: e.activation(out=condbf[:], in_=spv("cond").rearrange("p (k c) -> p k c", c=2),
                                     func=AF.Silu), writes=["condbf"])
    op("act", lambda e: e.activation(out=hclam[:], in_=spv("lam"), func=AF.Exp, scale=-1.0), writes=["hclam"])
    op("act", lambda e: e.activation(out=hclam[:], in_=hclam[:], func=AF.Ln, bias=1.0, scale=1.0),
       reads=["hclam"], writes=["hclam"])
    op("dve", lambda e: e.tensor_scalar(out=hclam[:], in0=hclam[:], scalar1=-4.0, scalar2=None, op0=ALU.mult),
       reads=["hclam"], writes=["hclam"])
    op("act", lambda e: e.activation(out=esink[:], in_=spv("sink"), func=AF.Exp), writes=["esink"])
    op("dve", lambda e: e.tensor_scalar(out=hba[:], in0=spv("lba"), scalar1=0.5, scalar2=None, op0=ALU.mult),
       writes=["hba"])
    op("dve", lambda e: e.tensor_scalar(out=hbi[:], in0=spv("lbi"), scalar1=0.5, scalar2=None, op0=ALU.mult),
       writes=["hbi"])
    op("dve", lambda e: e.tensor_scalar(out=hbg[:], in0=spv("bgate"), scalar1=0.5, scalar2=None, op0=ALU.mult),
       writes=["hbg"])
    op("dve", lambda e: e.memset(sout[:], 0.0), writes=["sout"])

    def emit_mod_piece(l, pc):
        slot, (wv,) = wpiece([(0, [128, KC, 768], wsrc_rows(w_ada[l], pc * 768, (pc + 1) * 768))])
        sub = pc // 4
        for mc in range(6):
            ch = pc * 6 + mc
            pairs = [(wv[:, k, mc * 128:(mc + 1) * 128], condbf[:, k, :]) for k in range(KC)]
            edge = (pc % 4 == 0 and mc == 0) or (pc % 4 == 3 and mc == 5)
            mm_group(ps[7][:, ch * 2: ch * 2 + 2], pairs, reads=[("w", slot), "condbf"],
                     writes=[("ps7", sub)] if edge else [])
        if pc % 4 == 3:
            emit_mod_fin_sub(l, sub)

    def emit_mod_fin_sub(l, i):
        o_b, _ = SP_OFF["bada"]
        c0 = i * 24
        op("dve", lambda e: e.tensor_tensor(
            out=modT[:, l, c0:c0 + 24, :], in0=ps[7][:, c0 * 2:(c0 + 24) * 2].rearrange("p (c n) -> p c n", n=2),
            in1=bcast_last(smallp[:, o_b + l * 72 + c0: o_b + l * 72 + c0 + 24], 2), op=ALU.add),
           reads=[("ps7", i)], writes=[("modT", l, i)])
        o_n, _ = SP_OFF["normw"]
        nw = smallp[:, o_n + l * 24 + i * 8: o_n + l * 24 + i * 8 + 8]
        op("dve", lambda e: e.tensor_scalar(
            out=modA[:, l, i, :, :], in0=modT[:, l, i * 24 + 8: i * 24 + 16, :], scalar1=1.0, scalar2=None,
            op0=ALU.add), reads=[("modT", l, i)], writes=[("modA", l, i)])
        op("dve", lambda e: e.tensor_tensor(
            out=modA[:, l, i, :, :], in0=modA[:, l, i, :, :], in1=bcast_last(nw, 2), op=ALU.mult),
           reads=[("modA", l, i)], writes=[("modA", l, i)])
        op("dve", lambda e: e.tensor_scalar(
            out=modG[:, l, i, :, :], in0=modT[:, l, i * 24 + 16: i * 24 + 24, :], scalar1=0.5, scalar2=None,
            op0=ALU.mult), reads=[("modT", l, i)], writes=[("modG", l, i)])

    for pc_ in range(4):
        emit_mod_piece(0, pc_)
    modq0 = list(range(4, 12))

    def mod0_hook():
        if modq0:
            emit_mod_piece(0, modq0.pop(0))

    def cidx(tile):
        return 0 if tile < 2 else 1

    def tsl(tile):
        return slice(tile * TT, (tile + 1) * TT)

    def norm_temps(off):
        return {"sq": [carve(off + j * KB, [128, TT], BF16) for j in range(4)],
                "xn": [carve(off + 4 * KB + j * 2 * KB, [128, TT], F32) for j in range(2)],
                "sd": carve(off + 8 * KB, [128, TT], F32), "key": off}

    def modnorm(l, i, tile, h_out, T, A_ap=None):
        ci = cidx(tile)
        key = T["key"]
        sd = T["sd"]
        for k in range(KC):
            sqb = T["sq"][k % 4]
            if k % 2 == 0:
                op("act", lambda e, k=k, sqb=sqb: e.activation(out=sqb, in_=x[:, k, tsl(tile)], func=AF.Square),
                   reads=[("x", tile)], writes=[("sq", key, k % 4)])
            else:
                op("dve", lambda e, k=k, sqb=sqb: e.tensor_tensor(out=sqb, in0=x[:, k, tsl(tile)],
                                                                  in1=x[:, k, tsl(tile)], op=ALU.mult),
                   reads=[("x", tile)], writes=[("sq", key, k % 4)])
            first = (k == 0)
            last = (k == KC - 1)
            op("pe", lambda e, sqb=sqb, first=first, last=last: e.matmul(ps[6][:, :], lhsT=ones[:], rhs=sqb,
                                                                         start=first, stop=last),
               reads=[("sq", key, k % 4)], writes=[PS(6)] if (first or last) else [])
        op("act", lambda e: e.activation(out=sd, in_=ps[6][:, :], func=AF.Sqrt, bias=EPS, scale=1.0 / D),
           reads=[PS(6)], writes=[("sd", key)])
        op("dve", lambda e: e.reciprocal(out=sd, in_=sd), reads=[("sd", key)], writes=[("sd", key)])
        for k in range(KC):
            xb = T["xn"][k % 2]
            op("dve", lambda e, k=k, xb=xb: e.tensor_tensor(out=xb, in0=x[:, k, tsl(tile)], in1=sd, op=ALU.mult),
               reads=[("x", tile), ("sd", key)], writes=[("xn", key, k % 2)])
            if A_ap is None:
                a_ = modA[:, l, i, k, ci:ci + 1]
                b_ = modT[:, l, i * 24 + k, ci:ci + 1]
            else:
                a_ = A_ap[:, k:k + 1]
                b_ = 0.0
            op("act", lambda e, k=k, a_=a_, b_=b_, xb=xb: e.activation(out=h_out[:, k, :], in_=xb,
                                                                     func=AF.Identity, bias=b_, scale=a_),
               reads=[("xn", key, k % 2)], writes=[("h", tile)])

    def ffn(l, i, wset, hook=None, prenormed=False, after_tile=None):
        wg, wu, wd = wset
        h = carve(0, [128, KC, NTOK], BF16)
        act = carve(24 * KB, [128, 11, NTOK], BF16)
        NT_ = norm_temps(57 * KB)
        sg = [carve(83 * KB, [128, TT], F32), carve(85 * KB, [128, TT], F32)]
        if not prenormed:
            for tile in range(NTILE):
                modnorm(l, i, tile, h[:, :, tsl(tile)], NT_)
        cnt = 0
        for half in range(2):
            jl0 = 0
            for nj in (2, 2, 2, 2, 2, 1):
                c0 = (half * 11 + jl0) * 128
                slot, (gv, uv) = wpiece([(0, [128, KC, nj * 128], wsrc_rows(wg[l], c0, c0 + nj * 128)),
                                         (KC * 256, [128, KC, nj * 128], wsrc_rows(wu[l], c0, c0 + nj * 128))])
                for tile in range(NTILE):
                    for jj in range(nj):
                        jl = jl0 + jj
                        bg, bu = (0, 1) if cnt % 2 == 0 else (2, 3)
                        s_ = sg[cnt % 2]
                        cnt += 1
                        mm_group(ps[bg][:, :], [(gv[:, k, jj * 128:(jj + 1) * 128], h[:, k, tsl(tile)])
                                                for k in range(KC)],
                                 reads=[("w", slot), ("h", tile)], writes=[PS(bg)])
                        mm_group(ps[bu][:, :], [(uv[:, k, jj * 128:(jj + 1) * 128], h[:, k, tsl(tile)])
                                                for k in range(KC)],
                                 reads=[("w", slot), ("h", tile)], writes=[PS(bu)])
                        op("act", lambda e, s_=s_, bg=bg: e.activation(out=s_, in_=ps[bg][:, :], func=AF.Silu),
                           reads=[PS(bg)], writes=[("sg", id(s_))])
                        op("dve", lambda e, s_=s_, bu=bu, jl=jl, tile=tile: e.tensor_tensor(
                            out=act[:, jl, tsl(tile)], in0=s_, in1=ps[bu][:, :], op=ALU.mult),
                           reads=[("sg", id(s_)), PS(bu)], writes=[("act", tile)])
                jl0 += nj
                if hook is not None:
                    hook()
            r0 = half * 11 * 128

            def down(cp, tile, slot, dv):
                ci = cidx(tile)
                for mm_ in range(4):
                    m = cp * 4 + mm_
                    bd = 4 + (m % 2)
                    mm_group(ps[bd][:, :], [(dv[:, j, mm_ * 128:(mm_ + 1) * 128], act[:, j, tsl(tile)])
                                            for j in range(11)],
                             reads=[("w", slot), ("act", tile)], writes=[PS(bd)])
                    op("dve", lambda e, m=m, bd=bd, tile=tile, ci=ci: e.scalar_tensor_tensor(
                        out=x[:, m, tsl(tile)], in0=ps[bd][:, :], scalar=modG[:, l, i, m, ci:ci + 1],
                        in1=x[:, m, tsl(tile)], op0=ALU.mult, op1=ALU.add),
                       reads=[PS(bd), ("x", tile)], writes=[("x", tile)])

            def wd_piece(cp):
                src = wd[l][r0: r0 + 11 * 128, cp * 512:(cp + 1) * 512].rearrange("(j p) c -> p j c", p=128)
                slot, (dv,) = wpiece([(0, [128, 11, 512], src)])
                return slot, dv
            if half == 1 and after_tile is not None:
                pcs = [wd_piece(0), wd_piece(1)]
                for tile in range(NTILE):
                    for cp in range(2):
                        down(cp, tile, pcs[cp][0], pcs[cp][1])
                    after_tile(tile)
            else:
                for cp in range(2):
                    slot, dv = wd_piece(cp)
                    for tile in range(NTILE):
                        down(cp, tile, slot, dv)

    def phase_barrier():
        evs_ = []
        for en in ("pe", "act", "dve"):
            e_ = bld.engs[en]
            if e_.cnt > 0:
                evs_.append(Ev(e_.sem, e_.cnt))
        for k, (sem, cnt) in bld.dsems.items():
            if (isinstance(k, tuple) and k[0] == "w") or k == "cc":
                continue
            evs_.append(Ev(sem, cnt))
        bld.barrier(evs_, engines=("pe", "act", "dve", "sp"))

    def attention(l, qT, q0, keyblocks, PT, atoks, dens, attnT, oc0, ctr, pend):
        nb = len(keyblocks)
        par = ctr[1] % 2
        ctr[1] += 1
        atok = atoks[par]
        den = dens[par]
        obanks = (3, 4) if par == 0 else (6, 7)
        pset = par % len(PT)
        PTreg = ("PT", pset)
        PT = PT[pset]
        regs_all = []
        for kb in keyblocks:
            regs_all += list(kb[3])
        for bi, (k_ap, v_ap, m_ap, regs) in enumerate(keyblocks):
            for g in range(2):
                q3 = qT[g * 64:(g + 1) * 64, :, q0:q0 + 128]
                bk = ctr[0] % 3
                ctr[0] += 1
                out3 = ps[bk][:, :].rearrange("p (a b) -> p a b", a=4)
                pairs = [(k_ap[g * 64:(g + 1) * 64, :], q3)]
                if m_ap is not None:
                    pairs.append((ident[:], m_ap))
                mm_group(out3, pairs, reads=list(regs) + ["qT"], writes=[PS(bk)])
                op("act", lambda e, bk=bk, bi=bi, g=g: e.activation(out=PT[g][:, bi, :], in_=ps[bk][:, :],
                                                                   func=AF.Exp, scale=0.125),
                   reads=[PS(bk)], writes=[PTreg + (g,)])
        while pend:
            pend.pop(0)()
        for g in range(2):
            ob = obanks[g]
            for hh in range(4):
                pairs = [(PT[g][:, bi, hh * 128:(hh + 1) * 128], keyblocks[bi][1][:, g, :]) for bi in range(nb)]
                mm_group(ps[ob][:, hh * 65:(hh + 1) * 65], pairs, reads=[PTreg + (g,)] + regs_all,
                         writes=[PS(ob)] if hh in (0, 3) else [])
            o3 = ps[ob][:, 0:260].rearrange("p (h d) -> p h d", h=4)
            op("dve", lambda e, g=g, o3=o3: e.tensor_tensor(out=den[:, g * 4:(g + 1) * 4], in0=o3[:, :, 64],
                                                         in1=esink[:, l * 8 + g * 4: l * 8 + g * 4 + 4],
                                                         op=ALU.add),
               reads=[PS(ob)], writes=[("den", par, g)])
            op("dve", lambda e, g=g: e.reciprocal(out=den[:, g * 4:(g + 1) * 4], in_=den[:, g * 4:(g + 1) * 4]),
               reads=[("den", par, g)], writes=[("den", par, g)])
            op("dve", lambda e, g=g, o3=o3: e.tensor_tensor(
                out=atok[:, g * 256:(g + 1) * 256].rearrange("p (h d) -> p h d", h=4), in0=o3[:, :, 0:64],
                in1=bcast_last(den[:, g * 4:(g + 1) * 4], 64), op=ALU.mult),
               reads=[PS(ob), ("den", par, g)], writes=[("atok", par)])

        def fin():
            tpv = ps[5][:, 0:256].bitcast(BF16)
            for c4 in range(4):
                op("pe", lambda e, c4=c4: e.transpose(tpv[:, c4 * 128:(c4 + 1) * 128],
                                                      atok[:, c4 * 128:(c4 + 1) * 128], ident[:]),
                   reads=[("atok", par)], writes=[PS(5)])
            op("act", lambda e: e.activation(out=attnT[:, :, oc0:oc0 + 128],
                                             in_=tpv.rearrange("p (a b) -> p a b", a=4), func=AF.Copy),
               reads=[PS(5)], writes=["attnT"])
        pend.append(fin)

    def lru_chunk(l, c, L, xpads, segs, bufs, finish):
        xc, xcb, a_, s_, b_, h_, tr, ti = (bufs[k] for k in ("xc", "xcb", "a", "s", "b", "h", "tr", "ti"))
        o_w, _ = SP_OFF["convw"]
        o_cb, _ = SP_OFF["convb"]
        cw = [smallp[:, o_w + l * 16 + j * 4 + c: o_w + l * 16 + j * 4 + c + 1] for j in range(4)]
        cb = smallp[:, o_cb + l * 4 + c: o_cb + l * 4 + c + 1]
        for (d0, n, xp, regs) in xpads:
            op("dve", lambda e, d0=d0, n=n, xp=xp: e.tensor_scalar(
                out=xc[:, d0:d0 + n], in0=xp[:, 0:n], scalar1=cw[0], scalar2=cb, op0=ALU.mult, op1=ALU.add),
               reads=list(regs), writes=["xc"])
            for j in range(1, 4):
                op("dve", lambda e, d0=d0, n=n, xp=xp, j=j: e.scalar_tensor_tensor(
                    out=xc[:, d0:d0 + n], in0=xp[:, j:j + n], scalar=cw[j], in1=xc[:, d0:d0 + n],
                    op0=ALU.mult, op1=ALU.add), reads=list(regs) + ["xc"], writes=["xc"])
        op("act", lambda e: e.activation(out=xcb[:, 0:L], in_=xc[:, 0:L], func=AF.Copy), reads=["xc"],
           writes=["xcb"])
        for d in range(2):
            wa = lruw[:, d * 8 + 0 * 4 + c, :]
            wi = lruw[:, d * 8 + 1 * 4 + c, :]
            col = l * 8 + d * 4 + c
            for t in range(L // TT):
                sl = slice(t * TT, (t + 1) * TT)
                mm_group(ps[0][:, :], [(wa, xcb[:, sl])], reads=["xcb"], writes=[PS(0)])
                mm_group(ps[1][:, :], [(wi, xcb[:, sl])], reads=["xcb"], writes=[PS(1)])
                op("act", lambda e: e.activation(out=tr, in_=ps[0][:, :], func=AF.Tanh,
                                                 bias=hba[:, col:col + 1], scale=0.5),
                   reads=[PS(0)], writes=["tr"])
                op("act", lambda e: e.activation(out=ti, in_=ps[1][:, :], func=AF.Tanh,
                                                 bias=hbi[:, col:col + 1], scale=0.5),
                   reads=[PS(1)], writes=["ti"])
                op("act", lambda e, sl=sl: e.activation(out=a_[:, sl], in_=tr, func=AF.Exp,
                                                        bias=hclam[:, col:col + 1], scale=hclam[:, col:col + 1]),
                   reads=["tr"], writes=["a"])
                op("dve", lambda e, sl=sl: e.scalar_tensor_tensor(out=b_[:, sl], in0=ti, scalar=1.0,
                                                                  in1=xc[:, sl], op0=ALU.add, op1=ALU.mult),
                   reads=["ti", "xc"], writes=["b"])
            op("dve", lambda e: e.tensor_tensor(out=s_[:, 0:L], in0=a_[:, 0:L], in1=a_[:, 0:L], op=ALU.mult),
               reads=["a"], writes=["s"])
            op("act", lambda e: e.activation(out=s_[:, 0:L], in_=s_[:, 0:L], func=AF.Sqrt, bias=1.0, scale=-1.0),
               reads=["s"], writes=["s"])
            op("dve", lambda e: e.scalar_tensor_tensor(out=b_[:, 0:L], in0=s_[:, 0:L], scalar=0.5, in1=b_[:, 0:L],
                                                       op0=ALU.mult, op1=ALU.mult),
               reads=["s", "b"], writes=["b"])
            order = segs if d == 0 else list(reversed(segs))
            for (t0, n, initf, initb) in order:
                init = initf if d == 0 else initb
                aa, bb, hh_ = a_[:, t0:t0 + n], b_[:, t0:t0 + n], h_[:, t0:t0 + n]
                if d == 1:
                    aa, bb, hh_ = rev_ap(aa), rev_ap(bb), rev_ap(hh_)
                op("dve", lambda e, aa=aa, bb=bb, hh_=hh_, init=init: e.tensor_tensor_scan(
                    out=hh_, data0=aa, data1=bb, initial=init, op0=ALU.mult, op1=ALU.add),
                   reads=["a", "b"], writes=["hscan"])
            finish(d, h_)

    def lru_conv(l, c, L, conv_src, xcb, xcb_reg, dg, dg_reg, cnt, make_dg=True):
        o_w, _ = SP_OFF["convw"]
        o_cb, _ = SP_OFF["convb"]
        cw = [smallp[:, o_w + l * 16 + j * 4 + c: o_w + l * 16 + j * 4 + c + 1] for j in range(4)]
        cb = smallp[:, o_cb + l * 4 + c: o_cb + l * 4 + c + 1]
        if make_dg:
            for j in range(4):
                op("dve", lambda e, j=j: e.tensor_scalar(out=dg[:, j, :], in0=ident[:], scalar1=cw[j],
                                                         scalar2=None, op0=ALU.mult), writes=[dg_reg])
        for t in range(L // TT):
            cbk = 4 + (cnt[2] % 2)
            cnt[2] += 1
            pairs = []
            regs = []
            for j in range(4):
                rhs, oview, rg = conv_src(t, j)
                pairs.append((dg[:, j, :], rhs))
                regs = list(rg)
            mm_group(oview(ps[cbk][:, :]), pairs, reads=[dg_reg] + regs, writes=[PS(cbk)])
            op("act", lambda e, t=t, cbk=cbk: e.activation(out=xcb[:, t * TT:(t + 1) * TT], in_=ps[cbk][:, :],
                                                           func=AF.Identity, bias=cb, scale=1.0),
               reads=[PS(cbk)], writes=[xcb_reg])

    def lru_chunk2(l, c, L, xcb, xcb_reg, units, bufs, finish, cnt, mid_hook=None):
        carry = bufs["carry"]
        h_alias_s = bufs.get("h_alias_s", False)
        nset = bufs["nset"]
        for d in range(2):
            wa = lruw[:, d * 8 + 0 * 4 + c, :]
            wi = lruw[:, d * 8 + 1 * 4 + c, :]
            col = l * 8 + d * 4 + c
            uorder = units if d == 0 else list(reversed(units))
            for (u0, n, segs) in uorder:
                up = cnt[0] % nset
                cnt[0] += 1
                a_, s_, b_, h_ = (bufs[k][up] for k in ("a", "s", "b", "h"))
                for t in range(n // TT):
                    tp_ = cnt[1] % 2
                    cnt[1] += 1
                    tr, ti = bufs["tr"][tp_], bufs["ti"][tp_]
                    pb = 0 if tp_ == 0 else 2
                    sl = slice(t * TT, (t + 1) * TT)
                    gsl = slice(u0 + t * TT, u0 + (t + 1) * TT)
                    mm_group(ps[pb][:, :], [(wa, xcb[:, gsl])], reads=[xcb_reg, "lruw"], writes=[PS(pb)])
                    mm_group(ps[pb + 1][:, :], [(wi, xcb[:, gsl])], reads=[xcb_reg, "lruw"], writes=[PS(pb + 1)])
                    op("act", lambda e, tr=tr, pb=pb: e.activation(out=tr, in_=ps[pb][:, :], func=AF.Tanh,
                                                                   bias=hba[:, col:col + 1], scale=0.5),
                       reads=[PS(pb)], writes=[("tr", tp_)])
                    op("act", lambda e, ti=ti, pb=pb: e.activation(out=ti, in_=ps[pb + 1][:, :], func=AF.Tanh,
                                                                   bias=hbi[:, col:col + 1], scale=0.5),
                       reads=[PS(pb + 1)], writes=[("ti", tp_)])
                    op("act", lambda e, sl=sl, tr=tr, a_=a_: e.activation(out=a_[:, sl], in_=tr, func=AF.Exp,
                                                                          bias=hclam[:, col:col + 1],
                                                                          scale=hclam[:, col:col + 1]),
                       reads=[("tr", tp_)], writes=[("a", up)])
                    op("dve", lambda e, sl=sl, gsl=gsl, ti=ti, b_=b_: e.scalar_tensor_tensor(
                        out=b_[:, sl], in0=ti, scalar=1.0, in1=xcb[:, gsl], op0=ALU.add, op1=ALU.mult),
                       reads=[("ti", tp_), xcb_reg], writes=[("b", up)])
                op("dve", lambda e, n=n, a_=a_, s_=s_: e.tensor_tensor(out=s_[:, 0:n], in0=a_[:, 0:n],
                                                                       in1=a_[:, 0:n], op=ALU.mult),
                   reads=[("a", up)], writes=[("s", up), ("hscan", up)] if h_alias_s else [("s", up)])
                op("act", lambda e, n=n, s_=s_: e.activation(out=s_[:, 0:n], in_=s_[:, 0:n], func=AF.Sqrt,
                                                             bias=1.0, scale=-1.0),
                   reads=[("s", up)], writes=[("s", up)])
                op("dve", lambda e, n=n, s_=s_, b_=b_: e.scalar_tensor_tensor(
                    out=b_[:, 0:n], in0=s_[:, 0:n], scalar=0.5, in1=b_[:, 0:n], op0=ALU.mult, op1=ALU.mult),
                   reads=[("s", up), ("b", up)], writes=[("b", up)])
                sorder = segs if d == 0 else list(reversed(segs))
                for (t0, ns, initf, initb) in sorder:
                    init = initf if d == 0 else initb
                    rd = [("a", up), ("b", up)]
                    if isinstance(init, str):
                        init = carry[:, 0:1]
                        rd = rd + ["carry"]
                    aa, bb, hh_ = a_[:, t0:t0 + ns], b_[:, t0:t0 + ns], h_[:, t0:t0 + ns]
                    if d == 1:
                        aa, bb, hh_ = rev_ap(aa), rev_ap(bb), rev_ap(hh_)
                    op("dve", lambda e, aa=aa, bb=bb, hh_=hh_, init=init: e.tensor_tensor_scan(
                        out=hh_, data0=aa, data1=bb, initial=init, op0=ALU.mult, op1=ALU.add),
                       reads=rd, writes=[("hscan", up)])
                    last = (t0 + ns - 1) if d == 0 else t0
                    op("dve", lambda e, last=last, h_=h_: e.tensor_copy(out=carry[:, 0:1], in_=h_[:, last:last + 1]),
                       reads=[("hscan", up)], writes=["carry"])
                finish(d, u0, n, h_, ("hscan", up))
            if d == 0 and mid_hook is not None:
                mid_hook()

    def mixer(l, m1_prenormed=False):
        fourT = carve(0, [128, 4, NTOK], BF16)
        attnT = carve(12 * KB, [128, 4, NTOK], BF16)
        gy = carve(24 * KB, [128, 4, NTOK], BF16)
        qT = carve(36 * KB, [128, 4, NTOK], BF16)
        kT_p = carve(48 * KB, [128, TT], BF16)
        V_p = carve(49 * KB, [128, 4, 2, 65], BF16)
        xr_p = carve(51 * KB, [128, 4, 2, 260], BF16)
        xfp = carve(56 * KB, [128, 4, TT], BF16)
        kstage = carve(60 * KB, [128, TT], F32)
        vstage = carve(62 * KB, [128, 4, 128], F32)
        rt1 = carve(64 * KB, [128, TT], F32)
        rt2 = carve(66 * KB, [128, TT], F32)
        atok = carve(68 * KB, [128, 512], BF16)
        den = carve(69 * KB, [128, 8], F32)
        sd = carve(69 * KB + 64, [128, TT], F32)
        R5 = 72 * KB
        h1 = carve(0, [128, KC, NTOK], BF16)
        NT1 = norm_temps(R5 + 20 * KB)
        pay_sb = carve(R5, [128, PAYF], BF16)
        pay_k = pay_sb[:, 0:1024]
        pay_v = pay_sb[:, 1024:2048].rearrange("p (b c) -> p b c", b=8)
        pay_xr = pay_sb[:, 2048:6144].rearrange("p (c t) -> p c t", c=4)
        pay_xf = pay_sb[:, 6144:10240].rearrange("p (c t) -> p c t", c=4)
        W = w_in[l]

        phase_barrier()
        dma("pool", lruw[:], lruw_d[:, l * 16:(l + 1) * 16, :], writes=["lruw"], skey="lruw")
        op("dve", lambda e: e.memset(V_p[:], 1.0), writes=["V_p"])
        op("dve", lambda e: e.memset(xr_p[:], 0.0), writes=["xr_p"])

        if not m1_prenormed:
            for tile in range(NTILE):
                modnorm(l, 1, tile, h1[:, :, tsl(tile)], NT1)
        s0, (wq,) = wpiece([(0, [128, KC, 512], wsrc_rows(W, 0, 512))])
        s1, (wqs,) = wpiece([(0, [128, KC, 512], wsrc_rows(W, 512, 1024))])
        for tile in range(NTILE):
            hreg = ("h", tile)
            for c in range(4):
                mm_group(ps[0][:, :], [(wq[:, k, c * 128:(c + 1) * 128], h1[:, k, tsl(tile)]) for k in range(KC)],
                         reads=[("w", s0), hreg], writes=[PS(0)])
                if tile < 2:
                    mm_group(ps[1][:, :], [(wqs[:, k, c * 128:(c + 1) * 128], h1[:, k, tsl(tile)])
                                           for k in range(KC)],
                             reads=[("w", s1), hreg], writes=[PS(1)])
                    op("dve", lambda e, tile=tile: e.tensor_tensor(out=rt1, in0=ps[0][:, :],
                                                                   in1=rope[:, 0, tsl(tile)], op=ALU.mult),
                       reads=[PS(0)], writes=["rt1"])
                    op("dve", lambda e, tile=tile: e.tensor_tensor(out=rt2, in0=ps[1][:, :],
                                                                   in1=rope[:, 1, tsl(tile)], op=ALU.mult),
                       reads=[PS(1)], writes=["rt2"])
                    op("dve", lambda e, tile=tile, c=c: e.tensor_tensor(out=qT[:, c, tsl(tile)], in0=rt1, in1=rt2,
                                                                        op=ALU.add),
                       reads=["rt1", "rt2"], writes=["qT"])
                else:
                    op("act", lambda e, tile=tile, c=c: e.activation(out=qT[:, c, tsl(tile)], in_=ps[0][:, :],
                                                                     func=AF.Copy),
                       reads=[PS(0)], writes=["qT"])
        s2, (wk,) = wpiece([(0, [128, KC, 384], wsrc_rows(W, 1024, 1408))])
        for tile in range(NTILE):
            hreg = ("h", tile)
            mm_group(ps[0][:, :], [(wk[:, k, 0:128], h1[:, k, tsl(tile)]) for k in range(KC)],
                     reads=[("w", s2), hreg], writes=[PS(0)])
            if tile < 2:
                mm_group(ps[1][:, :], [(wk[:, k, 128:256], h1[:, k, tsl(tile)]) for k in range(KC)],
                         reads=[("w", s2), hreg], writes=[PS(1)])
                op("dve", lambda e, tile=tile: e.tensor_tensor(out=rt1, in0=ps[0][:, :],
                                                               in1=rope[:, 0, tsl(tile)], op=ALU.mult),
                   reads=[PS(0)], writes=["rt1"])
                op("dve", lambda e, tile=tile: e.tensor_tensor(out=rt2, in0=ps[1][:, :],
                                                               in1=rope[:, 1, tsl(tile)], op=ALU.mult),
                   reads=[PS(1)], writes=["rt2"])
                op("dve", lambda e, tile=tile: e.tensor_tensor(out=pay_k[:, tsl(tile)], in0=rt1, in1=rt2,
                                                               op=ALU.add),
                   reads=["rt1", "rt2"], writes=["pay"])
            else:
                op("act", lambda e: e.activation(out=kT_p, in_=ps[0][:, :], func=AF.Copy),
                   reads=[PS(0)], writes=["kT_p"])
                op("dve", lambda e: e.tensor_copy(out=kstage, in_=ps[0][:, :]), reads=[PS(0), "kT_p"],
                   writes=["kstage"])
                dma("sp", kout_d[l, :, :], kstage, reads=["kstage"], skey="kout", is_out=True)
            for b4 in range(4):
                blk = tile * 4 + b4
                bk = 2 + (blk % 2)
                mm_group(ps[bk][:, 0:128], [(h1[:, k, blk * 128:(blk + 1) * 128], wk[:, k, 256:384])
                                            for k in range(KC)],
                         reads=[("w", s2), hreg], writes=[PS(bk)])
                if tile < 2:
                    op("act", lambda e, blk=blk, bk=bk: e.activation(out=pay_v[:, blk, :], in_=ps[bk][:, 0:128],
                                                                     func=AF.Copy),
                       reads=[PS(bk)], writes=["pay"])
                else:
                    op("act", lambda e, b4=b4, bk=bk: e.activation(
                        out=V_p[:, b4, :, 0:64], in_=ps[bk][:, 0:128].rearrange("p (g d) -> p g d", g=2),
                        func=AF.Copy), reads=[PS(bk)], writes=["V_p"])
                    op("dve", lambda e, b4=b4, bk=bk: e.tensor_copy(out=vstage[:, b4, :], in_=ps[bk][:, 0:128]),
                       reads=[PS(bk), "V_p"], writes=["vstage"])
        dma("sp", vout_d[l, :, :, :], vstage, reads=["vstage"], skey="vout", is_out=True)
        for pi, c0 in enumerate((1408, 1920, 2432)):
            sP, (wv_,) = wpiece([(0, [128, KC, 512], wsrc_rows(W, c0, c0 + 512))])
            for tile in range(NTILE):
                hreg = ("h", tile)
                for c in range(4):
                    bk = c % 2
                    mm_group(ps[bk][:, :], [(wv_[:, k, c * 128:(c + 1) * 128], h1[:, k, tsl(tile)])
                                            for k in range(KC)],
                             reads=[("w", sP), hreg], writes=[PS(bk)])
                    if pi == 1:
                        op("act", lambda e, bk=bk, c=c, tile=tile: e.activation(
                            out=gy[:, c, tsl(tile)], in_=ps[bk][:, :], func=AF.Gelu_apprx_tanh),
                           reads=[PS(bk)], writes=["gy"])
                    elif tile < 2:
                        dst = (pay_xr if pi == 0 else pay_xf)[:, c, tsl(tile)]
                        op("act", lambda e, bk=bk, dst=dst: e.activation(out=dst, in_=ps[bk][:, :], func=AF.Copy),
                           reads=[PS(bk)], writes=["pay"])
                    elif pi == 0:
                        op("act", lambda e, bk=bk, c=c: e.activation(
                            out=xr_p[:, c, :, 2:258], in_=ps[bk][:, :].rearrange("p (s t) -> p s t", s=2),
                            func=AF.Copy), reads=[PS(bk)], writes=["xr_p"])
                    else:
                        op("act", lambda e, bk=bk, c=c: e.activation(out=xfp[:, c, :], in_=ps[bk][:, :],
                                                                     func=AF.Copy),
                           reads=[PS(bk)], writes=["xfp"])

        if STOP == "m1_%d" % l:
            return True
        pool = bld.engs["pool"]
        if "cc" not in bld.dsems:
            bld.dsems["cc"] = [nc.semaphore("cc_sem").__enter__(), 0]
        ccs = bld.dsems["cc"]
        for (pt, gt, c0, c1, pr, gr) in ((payA_t, gatA_t, 0, 6144, "payA_d", "gat_d"),
                                         (payB_t, gatB_t, 6144, 10240, "payB_d", "gatB_d")):
            dma("pool", pt.ap()[:, :], pay_sb[:, c0:c1], reads=["pay"], writes=[pr], skey=pr)
            for ev in bld._deps([pr], [gr]):
                bld.wait(pool, ev)
            ccs[1] += 1
            nc.gpsimd.collective_compute("AllGather", ALU.bypass,
                                         replica_groups=[[0, 1], [2, 3], [4, 5], [6, 7]],
                                         ins=[pt.ap().opt()], outs=[gt.ap().opt()]).then_inc(ccs[0], 1)
            bld._commit(Ev(ccs[0], ccs[1]), [pr], [gr])
        phase_barrier()

        if STOP == "m2_%d" % l:
            return True
        PT = [carve(R5, [128, 5, 512], BF16), carve(R5 + 5 * KB, [128, 5, 512], BF16)]
        K_pad = carve(R5 + 10 * KB, [128, 18 * 128], BF16)
        V_pad = carve(R5 + 15 * KB, [128, 18, 2, 65], BF16)
        K_ext = carve(R5 + 20 * KB, [128, 10 * 128], BF16)
        V_ext = carve(R5 + 23 * KB, [128, 10, 2, 65], BF16)
        o_s, _ = SP_OFF["sel"]
        sel0 = smallp[:, o_s:o_s + 1]
        sel1 = smallp[:, o_s + 1:o_s + 2]
        ctr = [0, 0]
        apend = []
        atoks = [atok, carve(R5 + 26 * KB, [128, 512], BF16)]
        dens = [den, carve(R5 + 27 * KB, [128, 8], F32)]

        for sq_ in range(2):
            kbs = []
            for kb in range(2):
                col = sq_ * 256 + kb * 128
                kbs.append((kT_p[:, col:col + 128], V_p[:, sq_ * 2 + kb, :, :], None, ["kT_p", "V_p"]))
            for qb in range(2):
                q0 = 1024 + sq_ * 256 + qb * 128
                attention(l, qT, q0, kbs, [PT], atoks, dens, attnT, q0, ctr, apend)

        while apend:
            apend.pop(0)()
        if STOP == "m3pa_%d" % l:
            return True
        ABp = carve(R5 + 28 * KB, [128, 4, 256], BF16)
        for g in range(4):
            for sq_ in range(2):
                for nb in range(2):
                    col = sq_ * 256 + nb * 128
                    hb_ = (sq_ * 2 + nb) % 2
                    mm_group(ps[0][:, hb_ * 256:hb_ * 256 + 256],
                             [(xfp[:, g, col:col + 128], dftc[:, :])], reads=["xfp"], writes=[PS(0)])
                    op("act", lambda e, sq_=sq_, nb=nb, hb_=hb_: e.activation(
                        out=ABp[:, sq_ * 2 + nb, :], in_=ps[0][:, hb_ * 256:hb_ * 256 + 256], func=AF.Copy),
                       reads=[PS(0)], writes=["ABp"])
            for sq_ in range(2):
                pairs = []
                for cs in range(2):
                    for nb in range(2):
                        pairs.append((ABp[:, sq_ * 2 + nb, cs * 128:(cs + 1) * 128], dftp[:, nb, cs, :]))
                mm_group(ps[1][:, sq_ * 256:(sq_ + 1) * 256], pairs, reads=["ABp"], writes=[PS(1)])
            op("act", lambda e, g=g: e.activation(out=fourT[:, g, 1024:1536], in_=ps[1][:, :], func=AF.Copy),
               reads=[PS(1)], writes=["fourT"])
        modq = list(range(12)) if l + 1 < DEPTH else []

        def mod_some(n):
            for _ in range(n):
                if modq:
                    emit_mod_piece(l + 1, modq.pop(0))
        lbp = {"xcb": carve(R5 + 24 * KB, [128, 512], BF16), "dg": carve(R5 + 30 * KB, [128, 4, 128], BF16),
               "a": [carve(R5 + 12 * KB, [128, 512], F32)], "s": [carve(R5 + 14 * KB, [128, 512], F32)],
               "b": [carve(R5 + 16 * KB, [128, 512], F32)], "h": [carve(R5 + 18 * KB, [128, 512], F32)],
               "tr": [rt1, carve(69 * KB + 64, [128, TT], F32)], "ti": [rt2, carve(R5 + 22 * KB, [128, TT], F32)],
               "carry": carve(R5 + 25 * KB, [128, 8], F32), "nset": 1}
        accp = carve(R5 + 20 * KB, [128, 512], F32)
        lcnt = [0, 0, 0]
        for c in range(4):
            def fin_p(d, u0, n, h_, hreg, c=c):
                if d == 0:
                    op("dve", lambda e: e.tensor_copy(out=accp[:, 0:512], in_=h_[:, 0:512]), reads=[hreg],
                       writes=["accp"])
                    for sq_ in range(2):
                        col = l * 16 + sq_ * 8 + 0 * 4 + c
                        op("dve", lambda e, col=col, sq_=sq_: e.tensor_copy(
                            out=sout[:, col:col + 1], in_=h_[:, sq_ * 256 + 255: sq_ * 256 + 256]),
                           reads=[hreg], writes=["sout"])
                else:
                    for sq_ in range(2):
                        col = l * 16 + sq_ * 8 + 1 * 4 + c
                        op("dve", lambda e, col=col, sq_=sq_: e.tensor_copy(
                            out=sout[:, col:col + 1], in_=h_[:, sq_ * 256: sq_ * 256 + 1]),
                           reads=[hreg], writes=["sout"])
                    op("dve", lambda e: e.tensor_tensor(out=accp[:, 0:512], in0=accp[:, 0:512], in1=h_[:, 0:512],
                                                        op=ALU.add), reads=[hreg, "accp"], writes=["accp"])
                    op("dve", lambda e: e.tensor_tensor(out=gy[:, c, 1024:1536], in0=accp[:, 0:512],
                                                        in1=gy[:, c, 1024:1536], op=ALU.mult),
                       reads=["accp", "gy"], writes=["gy"])
            def csrc_p(t, j, c=c):
                return (xr_p[:, c, :, j:j + 256], lambda pa: pa.rearrange("p (s t) -> p s t", s=2), ["xr_p"])
            lru_conv(l, c, 512, csrc_p, lbp["xcb"], "xcb", lbp["dg"], "dg", lcnt)
            lru_chunk2(l, c, 512, lbp["xcb"], "xcb", [(0, 512, [(0, 256, 0.0, 0.0), (256, 256, 0.0, 0.0)])],
                       lbp, fin_p, lcnt)
        phase_barrier()
        op("dve", lambda e: e.memset(K_pad, 0.0), writes=["K_pad"])
        op("dve", lambda e: e.memset(V_pad, 1.0), writes=["V_pad"])
        for r_ in range(2):
            dma("sp", K_pad[:, 128 + r_ * 1024: 128 + (r_ + 1) * 1024], gat_d[r_ * 128:(r_ + 1) * 128, 0:1024],
                reads=["gat_d"], writes=["K_pad"], skey="kpad")
            for g in range(2):
                dma("sp", V_pad[:, 1 + r_ * 8: 9 + r_ * 8, g, 0:64],
                    gat_d[r_ * 128:(r_ + 1) * 128, 1024:2048].rearrange("p (b g d) -> p b g d", b=8, g=2)[:, :, g, :],
                    reads=["gat_d"], writes=["V_pad"], skey="vpad")
        op("dve", lambda e: e.tensor_scalar(out=K_ext, in0=K_pad[:, 0:1280], scalar1=sel0, scalar2=None,
                                            op0=ALU.mult), reads=["K_pad"], writes=["K_ext"])
        op("dve", lambda e: e.scalar_tensor_tensor(out=K_ext, in0=K_pad[:, 1024:2304], scalar=sel1, in1=K_ext,
                                                   op0=ALU.mult, op1=ALU.add),
           reads=["K_pad", "K_ext"], writes=["K_ext"])
        op("dve", lambda e: e.tensor_scalar(out=V_ext, in0=V_pad[:, 0:10, :, :], scalar1=sel0, scalar2=None,
                                            op0=ALU.mult), reads=["V_pad"], writes=["V_ext"])
        op("dve", lambda e: e.scalar_tensor_tensor(out=V_ext, in0=V_pad[:, 8:18, :, :], scalar=sel1, in1=V_ext,
                                                   op0=ALU.mult, op1=ALU.add),
           reads=["V_pad", "V_ext"], writes=["V_ext"])
        PT2 = [carve(R5 + 10 * KB, [128, 5, 512], BF16), carve(R5 + 15 * KB, [128, 5, 512], BF16)]
        ctr[1] = 0
        for n in range(8):
            mL = masks[:, 0 if n == 0 else 1, :]
            mR = masks[:, 3 if n == 7 else 2, :]
            kbs = [
                (K_ext[:, n * 128:(n + 1) * 128], V_ext[:, n, :, :], bcast_mid(mL, 4), ["K_ext", "V_ext"]),
                (K_ext[:, (n + 1) * 128:(n + 2) * 128], V_ext[:, n + 1, :, :], None, ["K_ext", "V_ext"]),
                (K_ext[:, (n + 2) * 128:(n + 3) * 128], V_ext[:, n + 2, :, :], bcast_mid(mR, 4),
                 ["K_ext", "V_ext"]),
                (ctxk[:, l, 0:128], ctxv[:, l, 0, :, :], None, []),
                (ctxk[:, l, 128:256], ctxv[:, l, 1, :, :], None, []),
            ]
            attention(l, qT, n * 128, kbs, [PT, PT2], atoks, dens, attnT, n * 128, ctr, apend)
        while apend:
            apend.pop(0)()
        phase_barrier()

        if STOP == "m3sa_%d" % l:
            return True
        lb = {"xcb": carve(R5 + 8 * KB, [128, 2048], BF16), "dg": carve(R5 + 30 * KB, [128, 4, 128], BF16),
              "a": [carve(R5 + 12 * KB, [128, 512], F32)], "s": [carve(R5 + 14 * KB, [128, 512], F32)],
              "b": [carve(R5 + 16 * KB, [128, 512], F32)], "h": [carve(R5 + 18 * KB, [128, 512], F32)],
              "tr": [rt1, carve(69 * KB + 64, [128, TT], F32)], "ti": [rt2, carve(44 * KB + 256, [128, TT], F32)],
              "carry": den, "nset": 1}
        xrf = carve(36 * KB, [128, 2052], BF16)
        acc = carve(40 * KB + 256, [128, 1024], F32)
        o_h0, _ = SP_OFF["h0"]
        phase_barrier()
        lb["a"] = [carve(R5 + 12 * KB, [128, 2048], F32)]
        lb["s"] = [carve(R5 + 20 * KB, [128, 2048], F32)]
        lb["b"] = [carve(48 * KB, [128, 2048], F32)]
        lb["h"] = lb["s"]
        lb["h_alias_s"] = True
        xcbs = [lb["xcb"], carve(56 * KB, [128, 2048], BF16)]
        dga = carve(R5 + 28 * KB, [128, 4, 4, 128], BF16)
        o_w_, _ = SP_OFF["convw"]
        for c_ in range(4):
            for j_ in range(4):
                cwj = smallp[:, o_w_ + l * 16 + j_ * 4 + c_: o_w_ + l * 16 + j_ * 4 + c_ + 1]
                op("dve", lambda e, c_=c_, j_=j_, cwj=cwj: e.tensor_scalar(
                    out=dga[:, c_, j_, :], in0=ident[:], scalar1=cwj, scalar2=None, op0=ALU.mult),
                   writes=["dga"])
        xrfs = [xrf, carve(R5, [128, 2052], BF16)]
        for xb_ in xrfs:
            op("dve", lambda e, xb_=xb_: e.memset(xb_[:, 0:2], 0.0), writes=[("xrf", id(xb_))])
            op("dve", lambda e, xb_=xb_: e.memset(xb_[:, 2050:2052], 0.0), writes=[("xrf", id(xb_))])
        def load_xrf(c):
            xrf_c = xrfs[c % 2]
            xreg = ("xrf", id(xrf_c))
            for r_ in range(2):
                dma("sp", xrf_c[:, 2 + r_ * 1024: 2 + (r_ + 1) * 1024],
                    gat_d[r_ * 128:(r_ + 1) * 128, 2048 + c * 1024: 2048 + (c + 1) * 1024],
                    reads=["gat_d"], writes=[xreg], skey=("xrf", c % 2))

        def conv_s(c):
            xrf_c = xrfs[c % 2]
            xreg = ("xrf", id(xrf_c))

            def csrc_s(t, j):
                return (xrf_c[:, t * TT + j: t * TT + j + TT], lambda pa: pa, [xreg])
            lru_conv(l, c, 2048, csrc_s, xcbs[c % 2], ("xcb", c % 2), dga[:, c, :, :], "dga", lcnt, make_dg=False)

        load_xrf(0)
        load_xrf(1)
        conv_s(0)
        for c in range(4):
            h0f = smallp[:, o_h0 + l * 8 + 0 * 4 + c: o_h0 + l * 8 + 0 * 4 + c + 1]
            h0b = smallp[:, o_h0 + l * 8 + 1 * 4 + c: o_h0 + l * 8 + 1 * 4 + c + 1]

            def fin_s(d, u0, n, h_, hreg, c=c):
                if d == 0:
                    op("dve", lambda e: e.tensor_scalar(out=acc, in0=h_[:, 0:1024], scalar1=sel0, scalar2=None,
                                                        op0=ALU.mult), reads=[hreg], writes=["acc"])
                else:
                    op("dve", lambda e: e.scalar_tensor_tensor(out=acc, in0=h_[:, 0:1024], scalar=sel0, in1=acc,
                                                               op0=ALU.mult, op1=ALU.add),
                       reads=[hreg, "acc"], writes=["acc"])
                op("dve", lambda e: e.scalar_tensor_tensor(out=acc, in0=h_[:, 1024:2048], scalar=sel1, in1=acc,
                                                           op0=ALU.mult, op1=ALU.add),
                   reads=[hreg, "acc"], writes=["acc"])
                if d == 1:
                    op("dve", lambda e: e.tensor_tensor(out=gy[:, c, 0:1024], in0=acc, in1=gy[:, c, 0:1024],
                                                        op=ALU.mult), reads=["acc", "gy"], writes=["gy"])
            units = [(0, 2048, [(0, 2048, h0f, h0b)])]

            def mid(c=c):
                if c + 1 < 4:
                    conv_s(c + 1)
                if c + 2 < 4:
                    pass
            lru_chunk2(l, c, 2048, xcbs[c % 2], ("xcb", c % 2), units, lb, fin_s, lcnt, mid_hook=mid)
            if c + 2 < 4:
                load_xrf(c + 2)
            mod_some(3)
        mod_some(12)
        phase_barrier()

        if STOP == "lru_%d" % l:
            return True
        ABall = carve(R5, [128, 16, 4, 256], BF16)
        xfg = carve(36 * KB, [128, 2048], BF16)
        tab = [carve(40 * KB, [128, 4, 512], BF16), carve(44 * KB, [128, 4, 512], BF16)]
        xfa = carve(48 * KB, [128, 4, 2048], BF16)
        for g in range(4):
            for r_ in range(2):
                dma("sp", xfa[:, g, r_ * 1024:(r_ + 1) * 1024],
                    gatB_d[r_ * 128:(r_ + 1) * 128, g * 1024: (g + 1) * 1024],
                    reads=["gatB_d"], writes=[("xfa", g)], skey=("xfa", g))
        for g in range(4):
            for nb in range(16):
                eng_ = "act" if nb % 2 == 0 else "dve"
                bk = nb % 2
                hb_ = (nb // 2) % 2
                mm_group(ps[bk][:, hb_ * 256:(hb_ + 1) * 256],
                         [(xfa[:, g, nb * 128:(nb + 1) * 128], dftc[:, :])], reads=[("xfa", g)], writes=[PS(bk)])
                if eng_ == "act":
                    op("act", lambda e, nb=nb, hb_=hb_, g=g, bk=bk: e.activation(
                        out=ABall[:, nb, g, :], in_=ps[bk][:, hb_ * 256:(hb_ + 1) * 256], func=AF.Copy),
                       reads=[PS(bk)], writes=["AB"])
                else:
                    op("dve", lambda e, nb=nb, hb_=hb_, g=g, bk=bk: e.tensor_copy(
                        out=ABall[:, nb, g, :], in_=ps[bk][:, hb_ * 256:(hb_ + 1) * 256]),
                       reads=[PS(bk)], writes=["AB"])
        tcnt = 0
        for kt in range(2):
            for cs in range(2):
                for nq in range(4):
                    tb = tab[tcnt % 2]
                    treg = ("tab", tcnt % 2)
                    tcnt += 1
                    dma("sp", tb, dfts_d[kt * 2 + cs, :, nq * 4:(nq + 1) * 4, :], writes=[treg], skey=treg)
                    first = (cs == 0 and nq == 0)
                    lastp = (cs == 1 and nq == 3)
                    for g in range(4):
                        pairs = [(ABall[:, nq * 4 + j, g, cs * 128:(cs + 1) * 128], tb[:, j, :]) for j in range(4)]
                        mm_group(ps[2 + g][:, :], pairs, reads=[treg, "AB"],
                                 writes=[PS(2 + g)] if (first or lastp) else [],
                                 first_start=first, last_stop=lastp)
            for g in range(4):
                op("act", lambda e, g=g, kt=kt: e.activation(out=fourT[:, g, kt * 512:(kt + 1) * 512],
                                                            in_=ps[2 + g][:, :], func=AF.Copy),
                   reads=[PS(2 + g)], writes=["fourT"])
        phase_barrier()

        if STOP == "four_%d" % l:
            return True
        h4 = carve(R5, [128, KC, NTOK], BF16)
        mgb = carve(36 * KB, [128, KC, NTOK], BF16)
        NT4 = norm_temps(60 * KB)
        tgs = carve(R5 + 24 * KB, [128, 3, TT], BF16)
        mt1 = carve(R5 + 27 * KB, [128, TT], F32)
        mt2 = carve(R5 + 29 * KB, [128, TT], F32)
        for tile in range(NTILE):
            modnorm(l, 1, tile, h4[:, :, tsl(tile)], NT4)
        branches = (attnT, gy, fourT)
        for m in range(KC):
            sG, (wg_,) = wpiece([(0, [128, KC, 384], w_gate[l, m, :, :, :])])
            sB, (wb_,) = wpiece([(0, [128, 12, 128], w_out[l, m, :, :, :])])
            for tile in range(NTILE):
                hreg = ("h", tile)
                for b_ in range(3):
                    mm_group(ps[b_][:, :], [(wg_[:, k, b_ * 128:(b_ + 1) * 128], h4[:, k, tsl(tile)])
                                            for k in range(KC)],
                             reads=[("w", sG), hreg], writes=[PS(b_)])
                    op("act", lambda e, b_=b_, m=m: e.activation(
                        out=tgs[:, b_, :], in_=ps[b_][:, :], func=AF.Tanh,
                        bias=hbg[:, l * 24 + b_ * 8 + m: l * 24 + b_ * 8 + m + 1], scale=0.5),
                       reads=[PS(b_)], writes=[("tgs", b_)])
                for b_ in range(3):
                    mm_group(ps[3 + b_][:, :], [(wb_[:, b_ * 4 + k, :], branches[b_][:, k, tsl(tile)])
                                                for k in range(4)],
                             reads=[("w", sB), "attnT", "gy", "fourT"], writes=[PS(3 + b_)])
                op("dve", lambda e: e.scalar_tensor_tensor(out=mt1, in0=tgs[:, 0, :], scalar=1.0,
                                                           in1=ps[3][:, :], op0=ALU.add, op1=ALU.mult),
                   reads=[("tgs", 0), PS(3)], writes=["mt1"])
                op("dve", lambda e: e.scalar_tensor_tensor(out=mt2, in0=tgs[:, 1, :], scalar=1.0,
                                                           in1=ps[4][:, :], op0=ALU.add, op1=ALU.mult),
                   reads=[("tgs", 1), PS(4)], writes=["mt2"])
                op("dve", lambda e: e.tensor_tensor(out=mt1, in0=mt1, in1=mt2, op=ALU.add),
                   reads=["mt1", "mt2"], writes=["mt1"])
                op("dve", lambda e: e.scalar_tensor_tensor(out=mt2, in0=tgs[:, 2, :], scalar=1.0,
                                                           in1=ps[5][:, :], op0=ALU.add, op1=ALU.mult),
                   reads=[("tgs", 2), PS(5)], writes=["mt2"])
                op("dve", lambda e, m=m, tile=tile: e.tensor_tensor(out=mgb[:, m, tsl(tile)], in0=mt1, in1=mt2,
                                                                    op=ALU.add),
                   reads=["mt1", "mt2"], writes=[("mgb", tile)])
        wos = []
        for hp in range(2):
            sO, (wo_,) = wpiece([(0, [128, KC, 512], wsrc_rows(w_o[l], hp * 512, (hp + 1) * 512))])
            wos.append((sO, wo_))
        hF = carve(0, [128, KC, NTOK], BF16)
        for tile in range(NTILE):
            ci = cidx(tile)
            for hp in range(2):
                sO, wo_ = wos[hp]
                for mm_ in range(4):
                    m = hp * 4 + mm_
                    bk = mm_ % 2
                    mm_group(ps[bk][:, :], [(wo_[:, k, mm_ * 128:(mm_ + 1) * 128], mgb[:, k, tsl(tile)])
                                            for k in range(KC)],
                             reads=[("w", sO), ("mgb", tile)], writes=[PS(bk)])
                    op("dve", lambda e, m=m, bk=bk, tile=tile, ci=ci: e.scalar_tensor_tensor(
                        out=x[:, m, tsl(tile)], in0=ps[bk][:, :], scalar=modG[:, l, 1, m, ci:ci + 1],
                        in1=x[:, m, tsl(tile)], op0=ALU.mult, op1=ALU.add),
                       reads=[PS(bk), ("x", tile)], writes=[("x", tile)])
            modnorm(l, 2, tile, hF[:, :, tsl(tile)], NT4)
        phase_barrier()


    fin_state = {"done": False}

    def next_norm_hook(l):
        NTn = norm_temps(57 * KB)
        hN = carve(0, [128, KC, NTOK], BF16)
        if l + 1 < DEPTH:
            return lambda tile: modnorm(l + 1, 0, tile, hN[:, :, tsl(tile)], NTn)
        ybs = [carve(67 * KB, [128, KC, TT], F32), carve(87 * KB, [128, KC, TT], F32),
               carve(0, [128, KC, TT], F32)]

        def fin(tile):
            yb = ybs[tile]
            modnorm(0, 0, tile, yb, NTn, A_ap=spv("fnw"))
            dma("sp", y_d[:, :, tsl(tile)], yb, reads=[("h", tile)], writes=[("yout", tile)],
                skey=("y", tile), is_out=True)
            fin_state["done"] = True
        return fin

    def run_all():
        for l in range(DEPTH):
            phase_barrier()
            NTm = norm_temps(57 * KB)
            hM = carve(0, [128, KC, NTOK], BF16)
            ffn(l, 0, ffn_w[1], hook=mod0_hook if l == 0 else None, prenormed=(l > 0 and not STOP),
                after_tile=None if STOP else (lambda tile, l=l: modnorm(l, 1, tile, hM[:, :, tsl(tile)], NTm)))
            while l == 0 and modq0:
                mod0_hook()
            if STOP == "ffn1_%d" % l:
                return
            if mixer(l, m1_prenormed=not STOP):
                return
            if STOP == "mix_%d" % l:
                return
            ffn(l, 2, ffn_w[2], prenormed=True, after_tile=None if STOP else next_norm_hook(l))
            if STOP == "ffn2_%d" % l:
                return

    if STOP != "pre":
        run_all()
    phase_barrier()
    if STOP:
        dbg_d = dout("dbg", [128, ARENA_ELEMS], BF16)
        dma("sp", dbg_d[:, :], arena[:], skey="dbgd", is_out=True)
        dma("sp", y_d[:, :, :], x[:], reads=[("x", 0), ("x", 1), ("x", 2)], skey="ydbg", is_out=True)
    else:
        assert fin_state["done"]
        ybs = []
    dma("sp", sout_d[:, :], sout[:], reads=["sout"], skey="soutd", is_out=True)
    sp = bld.engs["sp"]
    for ev in bld.out_evs:
        bld.wait(sp, ev)
    for en in ("pe", "act", "dve"):
        e_ = bld.engs[en]
        if e_.cnt:
            bld.wait(sp, Ev(e_.sem, e_.cnt))
    return nc


def _fm(v):
    v = np.asarray(v, np.float32)
    lead = v.shape[:-1]
    n = v.shape[-1] // 128
    v = v.reshape(lead + (n, 128))
    return np.moveaxis(v, -1, 0)


_NC_CACHE = {}


def kernel(**inp):
    inp = {k: np.asarray(v) for k, v in inp.items()}
    f32 = np.float32
    ident = np.eye(128, dtype=f32).astype(BF)
    ones = np.ones((128, 128), f32).astype(BF)
    jl = np.arange(128)[:, None]
    il = np.arange(128)[None, :]
    band_l = np.where(jl >= il, 0.0, MASKV).astype(f32)
    band_r = np.where(jl <= il, 0.0, MASKV).astype(f32)
    allm = np.full((128, 128), MASKV, f32)
    cj = np.arange(128)
    ang = 2.0 * np.pi * np.outer(cj, cj) / 128.0
    dftc = np.concatenate([np.cos(ang), np.sin(ang)], axis=1).astype(f32).astype(BF)
    n256 = np.arange(256)
    a256 = 2.0 * np.pi * np.outer(n256, n256) / 256.0
    nrm_p = 1.0 / np.sqrt(256.0 * 128.0)
    tp = np.stack([np.cos(a256) * nrm_p, -np.sin(a256) * nrm_p], axis=1)
    dftp = tp.reshape(2, 128, 2, 256).transpose(1, 0, 2, 3).astype(f32).astype(BF)
    nrm_s = 1.0 / np.sqrt(2048.0 * 128.0)
    n2048 = np.arange(2048, dtype=np.float64)

    def head_cols(h):
        return list(range(h * 64, (h + 1) * 64))

    def swap_d(cols):
        out = []
        for a in range(2):
            out += cols[a * 32 + 16: a * 32 + 32] + cols[a * 32: a * 32 + 16]
        return out
    qcols, qscols = [], []
    for j in range(4):
        for h in (j, 4 + j):
            qcols += head_cols(h)
            qscols += swap_d(head_cols(h))
    kcols = list(range(512, 640))
    kscols = swap_d(list(range(512, 576))) + swap_d(list(range(576, 640)))
    perm = qcols + qscols + kcols + kscols + list(range(640, 768)) + list(range(768, 2304))
    assert len(perm) == NCOLIN
    w_in_p = np.ascontiguousarray(inp["w_in"][:, :, perm])

    lruw = np.zeros((128, DEPTH * 16, 128), f32)
    for l in range(DEPTH):
        for d in range(2):
            for gi, wname in enumerate(("lru_wa", "lru_wi")):
                w = inp[wname][l, d]
                for c in range(4):
                    for bb in range(2):
                        lruw[bb * 64:(bb + 1) * 64, l * 16 + d * 8 + gi * 4 + c, bb * 64:(bb + 1) * 64] = w[c * 2 + bb]

    wg = inp["w_branch_gate"].reshape(DEPTH, KC, 128, 3, 8, 128)
    w_gate_p = np.ascontiguousarray(wg.transpose(0, 4, 2, 1, 3, 5)).reshape(DEPTH, 8, 128, KC, 384)
    wo3 = np.stack([inp["w_attn_out"], inp["w_lru_out"], inp["w_fourier_out"]], axis=1)
    wo3 = wo3.reshape(DEPTH, 3, 4, 128, 8, 128)
    w_out_p = np.ascontiguousarray(wo3.transpose(0, 4, 3, 1, 2, 5)).reshape(DEPTH, 8, 128, 12, 128)
    shared = {
        "ident": ident, "ones": ones, "dftc": dftc, "dftp": dftp, "lruw": lruw,
        "w_ada": inp["w_ada"], "ffn1_wg": inp["ffn1_wg"], "ffn1_wu": inp["ffn1_wu"], "ffn1_wd": inp["ffn1_wd"],
        "ffn2_wg": inp["ffn2_wg"], "ffn2_wu": inp["ffn2_wu"], "ffn2_wd": inp["ffn2_wd"],
        "w_in": w_in_p, "w_gate": w_gate_p, "w_out": w_out_p, "w_o": inp["w_o"],
    }
    inv = 10000.0 ** (-np.arange(0, 32, 2, dtype=np.float64) / 32.0)

    in_maps = []
    for i in range(8):
        s, r = i // 2, i % 2
        xs = inp["x_sample"][s, r * 1024:(r + 1) * 1024]
        xp = inp["x_prompt"][2 * i: 2 * i + 2].reshape(512, D)
        xall = np.concatenate([xs, xp], axis=0)
        xin = np.ascontiguousarray(xall.reshape(NTOK, KC, 128).transpose(2, 1, 0))
        sp_ = np.zeros((128, NS), f32)

        def put(name, arr):
            o, w = SP_OFF[name]
            arr = np.asarray(arr, f32).reshape(128, -1)
            sp_[:, o:o + arr.shape[1]] = arr
        cond = np.stack([_fm(inp["c"][s]), _fm(inp["c_ctx"])], axis=-1)
        put("cond", cond)
        put("bada", _fm(inp["b_ada"]).reshape(128, DEPTH * 72))
        put("normw", _fm(inp["norm_w"]).reshape(128, DEPTH * 24))
        put("fnw", _fm(inp["final_norm_w"]))
        put("bgate", _fm(inp["b_branch_gate"]).reshape(128, DEPTH * 24))
        put("convw", _fm(inp["conv_w"]).reshape(128, DEPTH * 16))
        put("convb", _fm(inp["conv_b"]).reshape(128, DEPTH * 4))
        put("lba", _fm(inp["lru_ba"]).reshape(128, DEPTH * 8))
        put("lbi", _fm(inp["lru_bi"]).reshape(128, DEPTH * 8))
        put("lam", _fm(inp["lru_lambda"]).reshape(128, DEPTH * 8))
        put("sink", np.broadcast_to(inp["attn_sink"].reshape(1, DEPTH * 8), (128, DEPTH * 8)))
        put("h0", _fm(inp["state_lru"][s]).reshape(128, DEPTH * 8))
        selv = np.zeros((128, 2), f32)
        selv[:, r] = 1.0
        put("sel", selv)
        masks = np.stack([allm if r == 0 else band_l, band_l, band_r, allm if r == 1 else band_r], axis=1)
        tg_ = r * 1024 + np.arange(1024)
        pos = np.stack([tg_ // 64, tg_ % 64], axis=0).astype(np.float64)
        C = np.zeros((64, 1024))
        S = np.zeros((64, 1024))
        for a in range(2):
            angl = inv[:, None] * pos[a][None, :]
            for half in range(2):
                rows = slice(a * 32 + half * 16, a * 32 + half * 16 + 16)
                C[rows] = np.cos(angl)
                S[rows] = np.sin(angl) * (-1.0 if half == 0 else 1.0)
        rope = np.stack([np.concatenate([C, C], 0), np.concatenate([S, S], 0)], axis=1).astype(f32).astype(BF)
        ctxk = np.ascontiguousarray(inp["cache_k"][s].reshape(DEPTH, 256, 128).transpose(2, 0, 1))
        ctxv = np.ascontiguousarray(
            inp["cache_v"][s].reshape(DEPTH, 2, 128, 128).transpose(2, 0, 1, 3))
        kk = (r * 1024 + np.arange(1024, dtype=np.float64))
        angs = 2.0 * np.pi * np.outer(n2048, kk) / 2048.0
        angs = np.mod(angs, 2.0 * np.pi)
        tabs = np.stack([np.cos(angs) * nrm_s, -np.sin(angs) * nrm_s], axis=0)
        tabs = tabs.reshape(2, 16, 128, 2, 512).transpose(3, 0, 2, 1, 4)
        dfts = np.ascontiguousarray(tabs.reshape(4, 128, 16, 512)).astype(f32).astype(BF)
        m = dict(shared)
        m.update({"xin": xin, "smallp": sp_, "masks": np.ascontiguousarray(masks).astype(BF), "rope": rope,
                  "ctxk": ctxk, "ctxv": ctxv, "dfts": dfts})
        in_maps.append(m)

    if "nc" not in _NC_CACHE:
        _NC_CACHE["nc"] = build()
    nc = _NC_CACHE["nc"]
    res = run_bass_kernel_spmd(nc, in_maps, core_ids=list(range(8)))
    outs = res.results
    _NC_CACHE["last"] = outs

    y_prompt = np.zeros((16, 256, D), f32)
    y_sample = np.zeros((4, 2048, D), f32)
    nk = np.zeros((16, DEPTH, 256, 2, 64), f32)
    nv = np.zeros((16, DEPTH, 256, 2, 64), f32)
    ns = np.zeros((16, DEPTH, 2, 512), f32)
    for i in range(8):
        s, r = i // 2, i % 2
        o = outs[i]
        y = np.asarray(o["y"], f32)
        yt = y.transpose(2, 1, 0).reshape(NTOK, D)
        y_sample[s, r * 1024:(r + 1) * 1024] = yt[:1024]
        y_prompt[2 * i: 2 * i + 2] = yt[1024:].reshape(2, 256, D)
        ko = np.asarray(o["kout"], f32)
        vo = np.asarray(o["vout"], f32)
        so = np.asarray(o["sout"], f32)
        for l in range(DEPTH):
            kt_ = ko[l].T.reshape(2, 256, 2, 64)
            vt_ = vo[l].transpose(1, 0, 2).reshape(512, 128).reshape(2, 256, 2, 64)
            for q in range(2):
                nk[2 * i + q, l] = kt_[q]
                nv[2 * i + q, l] = vt_[q]
                for d in range(2):
                    cols = so[:, l * 16 + q * 8 + d * 4: l * 16 + q * 8 + d * 4 + 4]
                    ns[2 * i + q, l, d] = cols.T.reshape(512)
    return (y_prompt, y_sample, nk, nv, ns)
```

```python
import os
import numpy as np
import ml_dtypes
import concourse.bass as bass
import concourse.mybir as mybir
from concourse.bass_utils import run_bass_kernel_spmd
from concourse.ap import AP

F32 = mybir.dt.float32
BF16 = mybir.dt.bfloat16
AF = mybir.ActivationFunctionType
ALU = mybir.AluOpType
BF = ml_dtypes.bfloat16

D = 1024
KC = 8
DFF = 2816
JC = 22
DEPTH = 2
NTOK = 1536
TT = 512
NTILE = 3
NCOLIN = 2944
PAYF = 10240
EPS = 1e-6
MASKV = -30000.0
RING_SLOTS = 3
SLOT_ELEMS = 6144

SP_OFF = {}
_o = 0
for _n, _w in [("cond", 16), ("bada", DEPTH * 144), ("normw", DEPTH * 24), ("fnw", 8), ("bgate", DEPTH * 24),
               ("convw", DEPTH * 16), ("convb", DEPTH * 4), ("lba", DEPTH * 8), ("lbi", DEPTH * 8),
               ("lam", DEPTH * 8), ("sink", DEPTH * 8), ("h0", DEPTH * 8), ("sel", 2)]:
    SP_OFF[_n] = (_o, _w)
    _o += _w
NS = _o


def bcast_last(ap, n):
    return AP(ap.tensor, ap.offset, [list(d) for d in ap.ap] + [[0, n]])


def bcast_mid(ap, n):
    a = [list(d) for d in ap.ap]
    return AP(ap.tensor, ap.offset, [a[0], [0, n]] + a[1:])


def rev_ap(ap):
    a = [list(d) for d in ap.ap]
    st, cnt = a[-1]
    return AP(ap.tensor, ap.offset + st * (cnt - 1), a[:-1] + [[-st, cnt]])


class Ev:
    __slots__ = ("sem", "val")

    def __init__(self, sem, val):
        self.sem = sem
        self.val = val


class Eng:
    def __init__(self, name, obj, sem):
        self.name = name
        self.obj = obj
        self.sem = sem
        self.cnt = 0
        self.seen = {}


class B:
    def __init__(self, nc):
        self.nc = nc
        self.lastw = {}
        self.readers = {}
        self.dsems = {}
        self.engs = {}
        for name, obj in [("pe", nc.tensor), ("act", nc.scalar), ("dve", nc.vector), ("pool", nc.gpsimd),
                          ("sp", nc.sync)]:
            sem = nc.semaphore("s_" + name).__enter__()
            self.engs[name] = Eng(name, obj, sem)
        self.out_evs = []

    def wait(self, eng, ev):
        k = id(ev.sem)
        if eng.seen.get(k, 0) >= ev.val:
            return
        eng.obj.wait_ge(ev.sem, ev.val)
        eng.seen[k] = ev.val

    def _deps(self, reads, writes):
        evs = []
        for k in reads:
            w = self.lastw.get(k)
            if w is not None:
                evs.append(w)
        for k in writes:
            w = self.lastw.get(k)
            if w is not None:
                evs.append(w)
            evs.extend(self.readers.get(k, ()))
        return evs

    def _commit(self, ev, reads, writes):
        for k in reads:
            self.readers.setdefault(k, []).append(ev)
        for k in writes:
            self.lastw[k] = ev
            self.readers[k] = []

    def op(self, en, fn, reads=(), writes=(), mark=True):
        eng = self.engs[en]
        for ev in self._deps(reads, writes):
            self.wait(eng, ev)
        ins = fn(eng.obj)
        if not mark:
            return None
        eng.cnt += 1
        ins.then_inc(eng.sem, 1)
        ev = Ev(eng.sem, eng.cnt)
        self._commit(ev, reads, writes)
        return ev

    def dma(self, en, out, in_, reads=(), writes=(), skey=None, is_out=False):
        eng = self.engs[en]
        for ev in self._deps(reads, writes):
            self.wait(eng, ev)
        if skey not in self.dsems:
            self.dsems[skey] = [self.nc.semaphore("d_%d" % len(self.dsems)).__enter__(), 0]
        ent = self.dsems[skey]
        ent[1] += 16
        eng.obj.dma_start(out=out, in_=in_).then_inc(ent[0], 16)
        ev = Ev(ent[0], ent[1])
        self._commit(ev, reads, writes)
        if is_out:
            self.out_evs.append(ev)
        return ev

    def barrier(self, evs, engines=("pe", "act", "dve", "pool", "sp")):
        for en in engines:
            for ev in evs:
                self.wait(self.engs[en], ev)


def build():
    nc = bass.Bass("TRN2", target_bir_lowering=False)

    def din(name, shape, dt=F32):
        return nc.dram_tensor(name, list(shape), dt, kind="ExternalInput").ap()

    def dout(name, shape, dt=F32):
        return nc.dram_tensor(name, list(shape), dt, kind="ExternalOutput").ap()

    xin = din("xin", [128, KC, NTOK])
    smallp_d = din("smallp", [128, NS])
    ident_d = din("ident", [128, 128], BF16)
    ones_d = din("ones", [128, 128], BF16)
    masks_d = din("masks", [128, 4, 128], BF16)
    rope_d = din("rope", [128, 2, 1024], BF16)
    ctxk_d = din("ctxk", [128, DEPTH, 256])
    ctxv_d = din("ctxv", [128, DEPTH, 2, 128])
    lruw_d = din("lruw", [128, DEPTH * 16, 128])
    dftc_d = din("dftc", [128, 256], BF16)
    dftp_d = din("dftp", [128, 2, 2, 256], BF16)
    dfts_d = din("dfts", [4, 128, 16, 512], BF16)
    w_ada = din("w_ada", [DEPTH, D, 9216])
    ffn_w = {}
    for i in (1, 2):
        ffn_w[i] = (din("ffn%d_wg" % i, [DEPTH, D, DFF]), din("ffn%d_wu" % i, [DEPTH, D, DFF]),
                    din("ffn%d_wd" % i, [DEPTH, DFF, D]))
    w_in = din("w_in", [DEPTH, D, NCOLIN])
    w_gate = din("w_gate", [DEPTH, 8, 128, KC, 384])
    w_out = din("w_out", [DEPTH, 8, 128, 12, 128])
    w_o = din("w_o", [DEPTH, D, D])

    y_d = dout("y", [128, KC, NTOK])
    kout_d = dout("kout", [DEPTH, 128, 512])
    vout_d = dout("vout", [DEPTH, 128, 4, 128])
    sout_d = dout("sout", [128, DEPTH * 16])

    payA_t = nc.dram_tensor("payA", [128, 6144], BF16)
    gatA_t = nc.dram_tensor("gatA", [256, 6144], BF16)
    payB_t = nc.dram_tensor("payB", [128, 4096], BF16)
    gatB_t = nc.dram_tensor("gatB", [256, 4096], BF16)
    gat_d = gatA_t.ap()
    gatB_d = gatB_t.ap()

    bld = B(nc)
    op, dma = bld.op, bld.dma
    STOP = os.environ.get("KSTOP", "")

    def sb(name, shape, dt):
        return nc.sbuf_tensor(name, list(shape), dt).__enter__()

    x = sb("x", [128, KC, NTOK], F32)
    ring = sb("ring", [128, RING_SLOTS * SLOT_ELEMS], BF16)
    smallp = sb("smallp_sb", [128, NS], F32)
    ident = sb("ident_sb", [128, 128], BF16)
    ones = sb("ones_sb", [128, 128], BF16)
    masks = sb("masks_sb", [128, 4, 128], BF16)
    rope = sb("rope_sb", [128, 2, 1024], BF16)
    ctxk = sb("ctxk_sb", [128, DEPTH, 256], BF16)
    ctxv = sb("ctxv_sb", [128, DEPTH, 2, 2, 65], BF16)
    lruw = sb("lruw_sb", [128, 16, 128], BF16)
    dftc = sb("dftc_sb", [128, 256], BF16)
    dftp = sb("dftp_sb", [128, 2, 2, 256], BF16)
    condbf = sb("condbf", [128, KC, 2], BF16)
    modT = sb("modT", [128, DEPTH, 72, 2], F32)
    modA = sb("modA", [128, DEPTH, 3, KC, 2], F32)
    modG = sb("modG", [128, DEPTH, 3, KC, 2], F32)
    hclam = sb("hclam", [128, DEPTH * 8], F32)
    esink = sb("esink", [128, DEPTH * 8], F32)
    hba = sb("hba", [128, DEPTH * 8], F32)
    hbi = sb("hbi", [128, DEPTH * 8], F32)
    hbg = sb("hbg", [128, DEPTH * 24], F32)
    sout = sb("sout_sb", [128, DEPTH * 16], F32)
    ARENA_ELEMS = 52 * 1024
    print("sbuf bytes remaining before arena:", nc.sbuf_bytes_remaining)
    arena = sb("arena", [128, ARENA_ELEMS], BF16)

    def carve(off_bytes, shape, dt):
        n = 1
        for s_ in shape[1:]:
            n *= s_
        esz = 4 if dt == F32 else 2
        assert off_bytes % 4 == 0
        assert off_bytes + n * esz <= ARENA_ELEMS * 2, (off_bytes, shape)
        v = arena[:, off_bytes // 2: off_bytes // 2 + n * esz // 2]
        if dt == F32:
            v = v.bitcast(F32)
        if len(shape) == 2:
            return v
        names = " ".join("d%d" % i for i in range(len(shape) - 1))
        kw = {"d%d" % i: shape[i + 1] for i in range(len(shape) - 2)}
        return v.rearrange("p (%s) -> p %s" % (names, names), **kw)

    KB = 1024

    def spv(name, a=0, b=None):
        o, w = SP_OFF[name]
        if b is None:
            b = w
        return smallp[:, o + a: o + b]

    ps = [nc.psum_tensor("ps%d" % i, [128, 512], F32).__enter__() for i in range(8)]

    def PS(i):
        return ("ps", i)

    wstate = {"n": 0}

    def wpiece(parts):
        slot = wstate["n"] % RING_SLOTS
        wstate["n"] += 1
        views = []
        for (eo, shape, src) in parts:
            n = 1
            for s_ in shape[1:]:
                n *= s_
            assert eo + n <= SLOT_ELEMS
            v = ring[:, slot * SLOT_ELEMS + eo: slot * SLOT_ELEMS + eo + n]
            if len(shape) == 3:
                v = v.rearrange("p (a b) -> p a b", a=shape[1])
            dma("pool", v, src, reads=(), writes=[("w", slot)], skey=("w", slot))
            views.append(v)
        return slot, views

    def wsrc_rows(w2d, c0, c1):
        return w2d[:, c0:c1].rearrange("(k p) c -> p k c", p=128)

    def mm_group(out_ap, pairs, reads, writes, first_start=True, last_stop=True):
        n = len(pairs)
        ev = None
        for idx, (l_, r_) in enumerate(pairs):
            st = first_start and idx == 0
            sp_ = last_stop and idx == n - 1
            if idx == 0 or idx == n - 1:
                ev = op("pe", lambda e, l_=l_, r_=r_, st=st, sp_=sp_: e.matmul(out_ap, lhsT=l_, rhs=r_, start=st,
                                                                             stop=sp_),
                        reads=reads, writes=writes)
            else:
                op("pe", lambda e, l_=l_, r_=r_: e.matmul(out_ap, lhsT=l_, rhs=r_, start=False, stop=False),
                   mark=False)
        return ev

    evs = []
    evs.append(dma("sp", smallp[:], smallp_d[:, :], writes=["smallp"], skey="c0"))
    evs.append(dma("sp", ident[:], ident_d[:, :], skey="c1"))
    evs.append(dma("sp", ones[:], ones_d[:, :], skey="c2"))
    evs.append(dma("sp", masks[:], masks_d[:, :, :], skey="c3"))
    evs.append(dma("sp", rope[:], rope_d[:, :, :], skey="c4"))
    evs.append(dma("sp", dftc[:], dftc_d[:, :], skey="c5"))
    evs.append(dma("sp", dftp[:], dftp_d[:, :, :, :], skey="c6"))
    evs.append(dma("sp", x[:], xin[:, :, :], writes=[("x", 0), ("x", 1), ("x", 2)], skey="c7"))
    evs.append(op("dve", lambda e: e.memset(ctxv[:], 1.0)))
    bld.barrier(evs[-1:], engines=("pool",))
    evs.append(dma("pool", ctxk[:], ctxk_d[:, :, :], skey="c8"))
    for l in range(DEPTH):
        for blk in range(2):
            evs.append(dma("pool", ctxv[:, l, blk, :, 0:64],
                           ctxv_d[:, l, blk, :].rearrange("p (g d) -> p g d", g=2), skey="c9%d%d" % (l, blk)))
    bld.barrier(evs)

    op("act", lambda e: e.activation(out=condbf[:], in_=spv("cond").rearrange("p (k c) -> p k c", c=2),
                                     func=AF.Silu), writes=["condbf"])
    op("act", lambda e: e.activation(out=hclam[:], in_=spv("lam"), func=AF.Exp, scale=-1.0), writes=["hclam"])
    op("act", lambda e: e.activation(out=hclam[:], in_=hclam[:], func=AF.Ln, bias=1.0, scale=1.0),
       reads=["hclam"], writes=["hclam"])
    op("dve", lambda e: e.tensor_scalar(out=hclam[:], in0=hclam[:], scalar1=-4.0, scalar2=None, op0=ALU.mult),
       reads=["hclam"], writes=["hclam"])
    op("act", lambda e: e.activation(out=esink[:], in_=spv("sink"), func=AF.Exp), writes=["esink"])
    op("dve", lambda e: e.tensor_scalar(out=hba[:], in0=spv("lba"), scalar1=0.5, scalar2=None, op0=ALU.mult),
       writes=["hba"])
    op("dve", lambda e: e.tensor_scalar(out=hbi[:], in0=spv("lbi"), scalar1=0.5, scalar2=None, op0=ALU.mult),
       writes=["hbi"])
    op("dve", lambda e: e.tensor_scalar(out=hbg[:], in0=spv("bgate"), scalar1=0.5, scalar2=None, op0=ALU.mult),
       writes=["hbg"])
    op("dve", lambda e: e.memset(sout[:], 0.0), writes=["sout"])

    def emit_mod_piece(l, pc):
        slot, (wv,) = wpiece([(0, [128, KC, 768], wsrc_rows(w_ada[l], pc * 768, (pc + 1) * 768))])
        sub = pc // 4
        for mc in range(6):
            ch = pc * 6 + mc
            pairs = [(wv[:, k, mc * 128:(mc + 1) * 128], condbf[:, k, :]) for k in range(KC)]
            edge = (pc % 4 == 0 and mc == 0) or (pc % 4 == 3 and mc == 5)
            mm_group(ps[7][:, ch * 2: ch * 2 + 2], pairs, reads=[("w", slot), "condbf"],
                     writes=[("ps7", sub)] if edge else [])
        if pc % 4 == 3:
            emit_mod_fin_sub(l, sub)

    def emit_mod_fin_sub(l, i):
        o_b, _ = SP_OFF["bada"]
        c0 = i * 24
        op("dve", lambda e: e.tensor_tensor(
            out=modT[:, l, c0:c0 + 24, :], in0=ps[7][:, c0 * 2:(c0 + 24) * 2].rearrange("p (c n) -> p c n", n=2),
            in1=bcast_last(smallp[:, o_b + l * 72 + c0: o_b + l * 72 + c0 + 24], 2), op=ALU.add),
           reads=[("ps7", i)], writes=[("modT", l, i)])
        o_n, _ = SP_OFF["normw"]
        nw = smallp[:, o_n + l * 24 + i * 8: o_n + l * 24 + i * 8 + 8]
        op("dve", lambda e: e.tensor_scalar(
            out=modA[:, l, i, :, :], in0=modT[:, l, i * 24 + 8: i * 24 + 16, :], scalar1=1.0, scalar2=None,
            op0=ALU.add), reads=[("modT", l, i)], writes=[("modA", l, i)])
        op("dve", lambda e: e.tensor_tensor(
            out=modA[:, l, i, :, :], in0=modA[:, l, i, :, :], in1=bcast_last(nw, 2), op=ALU.mult),
           reads=[("modA", l, i)], writes=[("modA", l, i)])
        op("dve", lambda e: e.tensor_scalar(
            out=modG[:, l, i, :, :], in0=modT[:, l, i * 24 + 16: i * 24 + 24, :], scalar1=0.5, scalar2=None,
            op0=ALU.mult), reads=[("modT", l, i)], writes=[("modG", l, i)])

    for pc_ in range(4):
        emit_mod_piece(0, pc_)
    modq0 = list(range(4, 12))

    def mod0_hook():
        if modq0:
            emit_mod_piece(0, modq0.pop(0))

    def cidx(tile):
        return 0 if tile < 2 else 1

    def tsl(tile):
        return slice(tile * TT, (tile + 1) * TT)

    def norm_temps(off):
        return {"sq": [carve(off + j * KB, [128, TT], BF16) for j in range(4)],
                "xn": [carve(off + 4 * KB + j * 2 * KB, [128, TT], F32) for j in range(2)],
                "sd": carve(off + 8 * KB, [128, TT], F32), "key": off}

    def modnorm(l, i, tile, h_out, T, A_ap=None):
        ci = cidx(tile)
        key = T["key"]
        sd = T["sd"]
        for k in range(KC):
            sqb = T["sq"][k % 4]
            if k % 2 == 0:
                op("act", lambda e, k=k, sqb=sqb: e.activation(out=sqb, in_=x[:, k, tsl(tile)], func=AF.Square),
                   reads=[("x", tile)], writes=[("sq", key, k % 4)])
            else:
                op("dve", lambda e, k=k, sqb=sqb: e.tensor_tensor(out=sqb, in0=x[:, k, tsl(tile)],
                                                                  in1=x[:, k, tsl(tile)], op=ALU.mult),
                   reads=[("x", tile)], writes=[("sq", key, k % 4)])
            first = (k == 0)
            last = (k == KC - 1)
            op("pe", lambda e, sqb=sqb, first=first, last=last: e.matmul(ps[6][:, :], lhsT=ones[:], rhs=sqb,
                                                                         start=first, stop=last),
               reads=[("sq", key, k % 4)], writes=[PS(6)] if (first or last) else [])
        op("act", lambda e: e.activation(out=sd, in_=ps[6][:, :], func=AF.Sqrt, bias=EPS, scale=1.0 / D),
           reads=[PS(6)], writes=[("sd", key)])
        op("dve", lambda e: e.reciprocal(out=sd, in_=sd), reads=[("sd", key)], writes=[("sd", key)])
        for k in range(KC):
            xb = T["xn"][k % 2]
            op("dve", lambda e, k=k, xb=xb: e.tensor_tensor(out=xb, in0=x[:, k, tsl(tile)], in1=sd, op=ALU.mult),
               reads=[("x", tile), ("sd", key)], writes=[("xn", key, k % 2)])
            if A_ap is None:
                a_ = modA[:, l, i, k, ci:ci + 1]
                b_ = modT[:, l, i * 24 + k, ci:ci + 1]
            else:
                a_ = A_ap[:, k:k + 1]
                b_ = 0.0
            op("act", lambda e, k=k, a_=a_, b_=b_, xb=xb: e.activation(out=h_out[:, k, :], in_=xb,
                                                                     func=AF.Identity, bias=b_, scale=a_),
               reads=[("xn", key, k % 2)], writes=[("h", tile)])

    def ffn(l, i, wset, hook=None, prenormed=False, after_tile=None):
        wg, wu, wd = wset
        h = carve(0, [128, KC, NTOK], BF16)
        act = carve(24 * KB, [128, 11, NTOK], BF16)
        NT_ = norm_temps(57 * KB)
        sg = [carve(83 * KB, [128, TT], F32), carve(85 * KB, [128, TT], F32)]
        if not prenormed:
            for tile in range(NTILE):
                modnorm(l, i, tile, h[:, :, tsl(tile)], NT_)
        cnt = 0
        for half in range(2):
            jl0 = 0
            for nj in (2, 2, 2, 2, 2, 1):
                c0 = (half * 11 + jl0) * 128
                slot, (gv, uv) = wpiece([(0, [128, KC, nj * 128], wsrc_rows(wg[l], c0, c0 + nj * 128)),
                                         (KC * 256, [128, KC, nj * 128], wsrc_rows(wu[l], c0, c0 + nj * 128))])
                for tile in range(NTILE):
                    for jj in range(nj):
                        jl = jl0 + jj
                        bg, bu = (0, 1) if cnt % 2 == 0 else (2, 3)
                        s_ = sg[cnt % 2]
                        cnt += 1
                        mm_group(ps[bg][:, :], [(gv[:, k, jj * 128:(jj + 1) * 128], h[:, k, tsl(tile)])
                                                for k in range(KC)],
                                 reads=[("w", slot), ("h", tile)], writes=[PS(bg)])
                        mm_group(ps[bu][:, :], [(uv[:, k, jj * 128:(jj + 1) * 128], h[:, k, tsl(tile)])
                                                for k in range(KC)],
                                 reads=[("w", slot), ("h", tile)], writes=[PS(bu)])
                        op("act", lambda e, s_=s_, bg=bg: e.activation(out=s_, in_=ps[bg][:, :], func=AF.Silu),
                           reads=[PS(bg)], writes=[("sg", id(s_))])
                        op("dve", lambda e, s_=s_, bu=bu, jl=jl, tile=tile: e.tensor_tensor(
                            out=act[:, jl, tsl(tile)], in0=s_, in1=ps[bu][:, :], op=ALU.mult),
                           reads=[("sg", id(s_)), PS(bu)], writes=[("act", tile)])
                jl0 += nj
                if hook is not None:
                    hook()
            r0 = half * 11 * 128

            def down(cp, tile, slot, dv):
                ci = cidx(tile)
                for mm_ in range(4):
                    m = cp * 4 + mm_
                    bd = 4 + (m % 2)
                    mm_group(ps[bd][:, :], [(dv[:, j, mm_ * 128:(mm_ + 1) * 128], act[:, j, tsl(tile)])
                                            for j in range(11)],
                             reads=[("w", slot), ("act", tile)], writes=[PS(bd)])
                    op("dve", lambda e, m=m, bd=bd, tile=tile, ci=ci: e.scalar_tensor_tensor(
                        out=x[:, m, tsl(tile)], in0=ps[bd][:, :], scalar=modG[:, l, i, m, ci:ci + 1],
                        in1=x[:, m, tsl(tile)], op0=ALU.mult, op1=ALU.add),
                       reads=[PS(bd), ("x", tile)], writes=[("x", tile)])

            def wd_piece(cp):
                src = wd[l][r0: r0 + 11 * 128, cp * 512:(cp + 1) * 512].rearrange("(j p) c -> p j c", p=128)
                slot, (dv,) = wpiece([(0, [128, 11, 512], src)])
                return slot, dv
            if half == 1 and after_tile is not None:
                pcs = [wd_piece(0), wd_piece(1)]
                for tile in range(NTILE):
                    for cp in range(2):
                        down(cp, tile, pcs[cp][0], pcs[cp][1])
                    after_tile(tile)
            else:
                for cp in range(2):
                    slot, dv = wd_piece(cp)
                    for tile in range(NTILE):
                        down(cp, tile, slot, dv)

    def phase_barrier():
        evs_ = []
        for en in ("pe", "act", "dve"):
            e_ = bld.engs[en]
            if e_.cnt > 0:
                evs_.append(Ev(e_.sem, e_.cnt))
        for k, (sem, cnt) in bld.dsems.items():
            if (isinstance(k, tuple) and k[0] == "w") or k == "cc":
                continue
            evs_.append(Ev(sem, cnt))
        bld.barrier(evs_, engines=("pe", "act", "dve", "sp"))

    def attention(l, qT, q0, keyblocks, PT, atoks, dens, attnT, oc0, ctr, pend):
        nb = len(keyblocks)
        par = ctr[1] % 2
        ctr[1] += 1
        atok = atoks[par]
        den = dens[par]
        obanks = (3, 4) if par == 0 else (6, 7)
        pset = par % len(PT)
        PTreg = ("PT", pset)
        PT = PT[pset]
        regs_all = []
        for kb in keyblocks:
            regs_all += list(kb[3])
        for bi, (k_ap, v_ap, m_ap, regs) in enumerate(keyblocks):
            for g in range(2):
                q3 = qT[g * 64:(g + 1) * 64, :, q0:q0 + 128]
                bk = ctr[0] % 3
                ctr[0] += 1
                out3 = ps[bk][:, :].rearrange("p (a b) -> p a b", a=4)
                pairs = [(k_ap[g * 64:(g + 1) * 64, :], q3)]
                mm_group(out3, pairs, reads=list(regs) + ["qT"], writes=[PS(bk)])
                op("act", lambda e, bk=bk, bi=bi, g=g: e.activation(out=PT[g][:, bi, :], in_=ps[bk][:, :],
                                                                   func=AF.Exp, scale=0.125),
                   reads=[PS(bk)], writes=[PTreg + (g,)])
                if m_ap is not None:
                    pv = PT[g][:, bi, :].rearrange("p (a b) -> p a b", a=4)
                    op("dve", lambda e, pv=pv, m_ap=m_ap: e.tensor_tensor(out=pv, in0=pv, in1=m_ap, op=ALU.mult),
                       reads=[PTreg + (g,)], writes=[PTreg + (g,)])
        while pend:
            pend.pop(0)()
        for g in range(2):
            ob = obanks[g]
            for hh in range(4):
                pairs = [(PT[g][:, bi, hh * 128:(hh + 1) * 128], keyblocks[bi][1][:, g, :]) for bi in range(nb)]
                mm_group(ps[ob][:, hh * 65:(hh + 1) * 65], pairs, reads=[PTreg + (g,)] + regs_all,
                         writes=[PS(ob)] if hh in (0, 3) else [])
            o3 = ps[ob][:, 0:260].rearrange("p (h d) -> p h d", h=4)
            op("dve", lambda e, g=g, o3=o3: e.tensor_tensor(out=den[:, g * 4:(g + 1) * 4], in0=o3[:, :, 64],
                                                         in1=esink[:, l * 8 + g * 4: l * 8 + g * 4 + 4],
                                                         op=ALU.add),
               reads=[PS(ob)], writes=[("den", par, g)])
            op("dve", lambda e, g=g: e.reciprocal(out=den[:, g * 4:(g + 1) * 4], in_=den[:, g * 4:(g + 1) * 4]),
               reads=[("den", par, g)], writes=[("den", par, g)])
            op("dve", lambda e, g=g, o3=o3: e.tensor_tensor(
                out=atok[:, g * 256:(g + 1) * 256].rearrange("p (h d) -> p h d", h=4), in0=o3[:, :, 0:64],
                in1=bcast_last(den[:, g * 4:(g + 1) * 4], 64), op=ALU.mult),
               reads=[PS(ob), ("den", par, g)], writes=[("atok", par)])

        def fin():
            tpv = ps[5][:, 0:256].bitcast(BF16)
            for c4 in range(4):
                op("pe", lambda e, c4=c4: e.transpose(tpv[:, c4 * 128:(c4 + 1) * 128],
                                                      atok[:, c4 * 128:(c4 + 1) * 128], ident[:]),
                   reads=[("atok", par)], writes=[PS(5)])
            op("act", lambda e: e.activation(out=attnT[:, :, oc0:oc0 + 128],
                                             in_=tpv.rearrange("p (a b) -> p a b", a=4), func=AF.Copy),
               reads=[PS(5)], writes=["attnT"])
        pend.append(fin)

    def lru_chunk(l, c, L, xpads, segs, bufs, finish):
        xc, xcb, a_, s_, b_, h_, tr, ti = (bufs[k] for k in ("xc", "xcb", "a", "s", "b", "h", "tr", "ti"))
        o_w, _ = SP_OFF["convw"]
        o_cb, _ = SP_OFF["convb"]
        cw = [smallp[:, o_w + l * 16 + j * 4 + c: o_w + l * 16 + j * 4 + c + 1] for j in range(4)]
        cb = smallp[:, o_cb + l * 4 + c: o_cb + l * 4 + c + 1]
        for (d0, n, xp, regs) in xpads:
            op("dve", lambda e, d0=d0, n=n, xp=xp: e.tensor_scalar(
                out=xc[:, d0:d0 + n], in0=xp[:, 0:n], scalar1=cw[0], scalar2=cb, op0=ALU.mult, op1=ALU.add),
               reads=list(regs), writes=["xc"])
            for j in range(1, 4):
                op("dve", lambda e, d0=d0, n=n, xp=xp, j=j: e.scalar_tensor_tensor(
                    out=xc[:, d0:d0 + n], in0=xp[:, j:j + n], scalar=cw[j], in1=xc[:, d0:d0 + n],
                    op0=ALU.mult, op1=ALU.add), reads=list(regs) + ["xc"], writes=["xc"])
        op("act", lambda e: e.activation(out=xcb[:, 0:L], in_=xc[:, 0:L], func=AF.Copy), reads=["xc"],
           writes=["xcb"])
        for d in range(2):
            wa = lruw[:, d * 8 + 0 * 4 + c, :]
            wi = lruw[:, d * 8 + 1 * 4 + c, :]
            col = l * 8 + d * 4 + c
            for t in range(L // TT):
                sl = slice(t * TT, (t + 1) * TT)
                mm_group(ps[0][:, :], [(wa, xcb[:, sl])], reads=["xcb"], writes=[PS(0)])
                mm_group(ps[1][:, :], [(wi, xcb[:, sl])], reads=["xcb"], writes=[PS(1)])
                op("act", lambda e: e.activation(out=tr, in_=ps[0][:, :], func=AF.Tanh,
                                                 bias=hba[:, col:col + 1], scale=0.5),
                   reads=[PS(0)], writes=["tr"])
                op("act", lambda e: e.activation(out=ti, in_=ps[1][:, :], func=AF.Tanh,
                                                 bias=hbi[:, col:col + 1], scale=0.5),
                   reads=[PS(1)], writes=["ti"])
                op("act", lambda e, sl=sl: e.activation(out=a_[:, sl], in_=tr, func=AF.Exp,
                                                        bias=hclam[:, col:col + 1], scale=hclam[:, col:col + 1]),
                   reads=["tr"], writes=["a"])
                op("dve", lambda e, sl=sl: e.scalar_tensor_tensor(out=b_[:, sl], in0=ti, scalar=1.0,
                                                                  in1=xc[:, sl], op0=ALU.add, op1=ALU.mult),
                   reads=["ti", "xc"], writes=["b"])
            op("dve", lambda e: e.tensor_tensor(out=s_[:, 0:L], in0=a_[:, 0:L], in1=a_[:, 0:L], op=ALU.mult),
               reads=["a"], writes=["s"])
            op("act", lambda e: e.activation(out=s_[:, 0:L], in_=s_[:, 0:L], func=AF.Sqrt, bias=1.0, scale=-1.0),
               reads=["s"], writes=["s"])
            op("dve", lambda e: e.scalar_tensor_tensor(out=b_[:, 0:L], in0=s_[:, 0:L], scalar=0.5, in1=b_[:, 0:L],
                                                       op0=ALU.mult, op1=ALU.mult),
               reads=["s", "b"], writes=["b"])
            order = segs if d == 0 else list(reversed(segs))
            for (t0, n, initf, initb) in order:
                init = initf if d == 0 else initb
                aa, bb, hh_ = a_[:, t0:t0 + n], b_[:, t0:t0 + n], h_[:, t0:t0 + n]
                if d == 1:
                    aa, bb, hh_ = rev_ap(aa), rev_ap(bb), rev_ap(hh_)
                op("dve", lambda e, aa=aa, bb=bb, hh_=hh_, init=init: e.tensor_tensor_scan(
                    out=hh_, data0=aa, data1=bb, initial=init, op0=ALU.mult, op1=ALU.add),
                   reads=["a", "b"], writes=["hscan"])
            finish(d, h_)

    def lru_conv(l, c, L, conv_src, xcb, xcb_reg, dg, dg_reg, cnt, make_dg=True):
        o_w, _ = SP_OFF["convw"]
        o_cb, _ = SP_OFF["convb"]
        cw = [smallp[:, o_w + l * 16 + j * 4 + c: o_w + l * 16 + j * 4 + c + 1] for j in range(4)]
        cb = smallp[:, o_cb + l * 4 + c: o_cb + l * 4 + c + 1]
        if make_dg:
            for j in range(4):
                op("dve", lambda e, j=j: e.tensor_scalar(out=dg[:, j, :], in0=ident[:], scalar1=cw[j],
                                                         scalar2=None, op0=ALU.mult), writes=[dg_reg])
        for t in range(L // TT):
            cbk = 4 + (cnt[2] % 2)
            cnt[2] += 1
            pairs = []
            regs = []
            for j in range(4):
                rhs, oview, rg = conv_src(t, j)
                pairs.append((dg[:, j, :], rhs))
                regs = list(rg)
            mm_group(oview(ps[cbk][:, :]), pairs, reads=[dg_reg] + regs, writes=[PS(cbk)])
            op("act", lambda e, t=t, cbk=cbk: e.activation(out=xcb[:, t * TT:(t + 1) * TT], in_=ps[cbk][:, :],
                                                           func=AF.Identity, bias=cb, scale=1.0),
               reads=[PS(cbk)], writes=[xcb_reg])

    def lru_chunk2(l, c, L, xcb, xcb_reg, units, bufs, finish, cnt, mid_hook=None):
        carry = bufs["carry"]
        h_alias_s = bufs.get("h_alias_s", False)
        nset = bufs["nset"]
        for d in range(2):
            wa = lruw[:, d * 8 + 0 * 4 + c, :]
            wi = lruw[:, d * 8 + 1 * 4 + c, :]
            col = l * 8 + d * 4 + c
            uorder = units if d == 0 else list(reversed(units))
            for (u0, n, segs) in uorder:
                up = cnt[0] % nset
                cnt[0] += 1
                a_, s_, b_, h_ = (bufs[k][up] for k in ("a", "s", "b", "h"))
                for t in range(n // TT):
                    tp_ = cnt[1] % 2
                    cnt[1] += 1
                    tr, ti = bufs["tr"][tp_], bufs["ti"][tp_]
                    pb = 0 if tp_ == 0 else 2
                    sl = slice(t * TT, (t + 1) * TT)
                    gsl = slice(u0 + t * TT, u0 + (t + 1) * TT)
                    mm_group(ps[pb][:, :], [(wa, xcb[:, gsl])], reads=[xcb_reg, "lruw"], writes=[PS(pb)])
                    mm_group(ps[pb + 1][:, :], [(wi, xcb[:, gsl])], reads=[xcb_reg, "lruw"], writes=[PS(pb + 1)])
                    op("act", lambda e, tr=tr, pb=pb: e.activation(out=tr, in_=ps[pb][:, :], func=AF.Tanh,
                                                                   bias=hba[:, col:col + 1], scale=0.5),
                       reads=[PS(pb)], writes=[("tr", tp_)])
                    op("act", lambda e, ti=ti, pb=pb: e.activation(out=ti, in_=ps[pb + 1][:, :], func=AF.Tanh,
                                                                   bias=hbi[:, col:col + 1], scale=0.5),
                       reads=[PS(pb + 1)], writes=[("ti", tp_)])
                    op("act", lambda e, sl=sl, tr=tr, a_=a_: e.activation(out=a_[:, sl], in_=tr, func=AF.Exp,
                                                                          bias=hclam[:, col:col + 1],
                                                                          scale=hclam[:, col:col + 1]),
                       reads=[("tr", tp_)], writes=[("a", up)])
                    op("dve", lambda e, sl=sl, gsl=gsl, ti=ti, b_=b_: e.scalar_tensor_tensor(
                        out=b_[:, sl], in0=ti, scalar=1.0, in1=xcb[:, gsl], op0=ALU.add, op1=ALU.mult),
                       reads=[("ti", tp_), xcb_reg], writes=[("b", up)])
                op("dve", lambda e, n=n, a_=a_, s_=s_: e.tensor_tensor(out=s_[:, 0:n], in0=a_[:, 0:n],
                                                                       in1=a_[:, 0:n], op=ALU.mult),
                   reads=[("a", up)], writes=[("s", up), ("hscan", up)] if h_alias_s else [("s", up)])
                op("act", lambda e, n=n, s_=s_: e.activation(out=s_[:, 0:n], in_=s_[:, 0:n], func=AF.Sqrt,
                                                             bias=1.0, scale=-1.0),
                   reads=[("s", up)], writes=[("s", up)])
                op("dve", lambda e, n=n, s_=s_, b_=b_: e.scalar_tensor_tensor(
                    out=b_[:, 0:n], in0=s_[:, 0:n], scalar=0.5, in1=b_[:, 0:n], op0=ALU.mult, op1=ALU.mult),
                   reads=[("s", up), ("b", up)], writes=[("b", up)])
                sorder = segs if d == 0 else list(reversed(segs))
                for (t0, ns, initf, initb) in sorder:
                    init = initf if d == 0 else initb
                    rd = [("a", up), ("b", up)]
                    if isinstance(init, str):
                        init = carry[:, 0:1]
                        rd = rd + ["carry"]
                    aa, bb, hh_ = a_[:, t0:t0 + ns], b_[:, t0:t0 + ns], h_[:, t0:t0 + ns]
                    if d == 1:
                        aa, bb, hh_ = rev_ap(aa), rev_ap(bb), rev_ap(hh_)
                    op("dve", lambda e, aa=aa, bb=bb, hh_=hh_, init=init: e.tensor_tensor_scan(
                        out=hh_, data0=aa, data1=bb, initial=init, op0=ALU.mult, op1=ALU.add),
                       reads=rd, writes=[("hscan", up)])
                    last = (t0 + ns - 1) if d == 0 else t0
                    op("dve", lambda e, last=last, h_=h_: e.tensor_copy(out=carry[:, 0:1], in_=h_[:, last:last + 1]),
                       reads=[("hscan", up)], writes=["carry"])
                finish(d, u0, n, h_, ("hscan", up))
            if d == 0 and mid_hook is not None:
                mid_hook()

    def mixer(l, m1_prenormed=False):
        fourT = carve(0, [128, 4, NTOK], BF16)
        attnT = carve(12 * KB, [128, 4, NTOK], BF16)
        gy = carve(24 * KB, [128, 4, NTOK], BF16)
        qT = carve(36 * KB, [128, 4, NTOK], BF16)
        kT_p = carve(48 * KB, [128, TT], BF16)
        V_p = carve(49 * KB, [128, 4, 2, 65], BF16)
        xr_p = carve(51 * KB, [128, 4, 2, 260], BF16)
        xfp = carve(56 * KB, [128, 4, TT], BF16)
        kstage = carve(60 * KB, [128, TT], F32)
        vstage = carve(62 * KB, [128, 4, 128], F32)
        rt1 = carve(64 * KB, [128, TT], F32)
        rt2 = carve(66 * KB, [128, TT], F32)
        atok = carve(68 * KB, [128, 512], BF16)
        den = carve(69 * KB, [128, 8], F32)
        sd = carve(69 * KB + 64, [128, TT], F32)
        R5 = 72 * KB
        h1 = carve(0, [128, KC, NTOK], BF16)
        NT1 = norm_temps(R5 + 20 * KB)
        pay_sb = carve(R5, [128, PAYF], BF16)
        pay_k = pay_sb[:, 0:1024]
        pay_v = pay_sb[:, 1024:2048].rearrange("p (b c) -> p b c", b=8)
        pay_xr = pay_sb[:, 2048:6144].rearrange("p (c t) -> p c t", c=4)
        pay_xf = pay_sb[:, 6144:10240].rearrange("p (c t) -> p c t", c=4)
        W = w_in[l]

        phase_barrier()
        dma("pool", lruw[:], lruw_d[:, l * 16:(l + 1) * 16, :], writes=["lruw"], skey="lruw")
        op("dve", lambda e: e.memset(V_p[:], 1.0), writes=["V_p"])
        op("dve", lambda e: e.memset(xr_p[:], 0.0), writes=["xr_p"])

        if not m1_prenormed:
            for tile in range(NTILE):
                modnorm(l, 1, tile, h1[:, :, tsl(tile)], NT1)
        s0, (wq,) = wpiece([(0, [128, KC, 512], wsrc_rows(W, 0, 512))])
        s1, (wqs,) = wpiece([(0, [128, KC, 512], wsrc_rows(W, 512, 1024))])
        for tile in range(NTILE):
            hreg = ("h", tile)
            for c in range(4):
                mm_group(ps[0][:, :], [(wq[:, k, c * 128:(c + 1) * 128], h1[:, k, tsl(tile)]) for k in range(KC)],
                         reads=[("w", s0), hreg], writes=[PS(0)])
                if tile < 2:
                    mm_group(ps[1][:, :], [(wqs[:, k, c * 128:(c + 1) * 128], h1[:, k, tsl(tile)])
                                           for k in range(KC)],
                             reads=[("w", s1), hreg], writes=[PS(1)])
                    op("dve", lambda e, tile=tile: e.tensor_tensor(out=rt1, in0=ps[0][:, :],
                                                                   in1=rope[:, 0, tsl(tile)], op=ALU.mult),
                       reads=[PS(0)], writes=["rt1"])
                    op("dve", lambda e, tile=tile: e.tensor_tensor(out=rt2, in0=ps[1][:, :],
                                                                   in1=rope[:, 1, tsl(tile)], op=ALU.mult),
                       reads=[PS(1)], writes=["rt2"])
                    op("dve", lambda e, tile=tile, c=c: e.tensor_tensor(out=qT[:, c, tsl(tile)], in0=rt1, in1=rt2,
                                                                        op=ALU.add),
                       reads=["rt1", "rt2"], writes=["qT"])
                else:
                    op("act", lambda e, tile=tile, c=c: e.activation(out=qT[:, c, tsl(tile)], in_=ps[0][:, :],
                                                                     func=AF.Copy),
                       reads=[PS(0)], writes=["qT"])
        s2, (wk,) = wpiece([(0, [128, KC, 384], wsrc_rows(W, 1024, 1408))])
        for tile in range(NTILE):
            hreg = ("h", tile)
            mm_group(ps[0][:, :], [(wk[:, k, 0:128], h1[:, k, tsl(tile)]) for k in range(KC)],
                     reads=[("w", s2), hreg], writes=[PS(0)])
            if tile < 2:
                mm_group(ps[1][:, :], [(wk[:, k, 128:256], h1[:, k, tsl(tile)]) for k in range(KC)],
                         reads=[("w", s2), hreg], writes=[PS(1)])
                op("dve", lambda e, tile=tile: e.tensor_tensor(out=rt1, in0=ps[0][:, :],
                                                               in1=rope[:, 0, tsl(tile)], op=ALU.mult),
                   reads=[PS(0)], writes=["rt1"])
                op("dve", lambda e, tile=tile: e.tensor_tensor(out=rt2, in0=ps[1][:, :],
                                                               in1=rope[:, 1, tsl(tile)], op=ALU.mult),
                   reads=[PS(1)], writes=["rt2"])
                op("dve", lambda e, tile=tile: e.tensor_tensor(out=pay_k[:, tsl(tile)], in0=rt1, in1=rt2,
                                                               op=ALU.add),
                   reads=["rt1", "rt2"], writes=["pay"])
            else:
                op("act", lambda e: e.activation(out=kT_p, in_=ps[0][:, :], func=AF.Copy),
                   reads=[PS(0)], writes=["kT_p"])
                op("dve", lambda e: e.tensor_copy(out=kstage, in_=ps[0][:, :]), reads=[PS(0), "kT_p"],
                   writes=["kstage"])
                dma("sp", kout_d[l, :, :], kstage, reads=["kstage"], skey="kout", is_out=True)
            for b4 in range(4):
                blk = tile * 4 + b4
                bk = 2 + (blk % 2)
                mm_group(ps[bk][:, 0:128], [(h1[:, k, blk * 128:(blk + 1) * 128], wk[:, k, 256:384])
                                            for k in range(KC)],
                         reads=[("w", s2), hreg], writes=[PS(bk)])
                if tile < 2:
                    op("act", lambda e, blk=blk, bk=bk: e.activation(out=pay_v[:, blk, :], in_=ps[bk][:, 0:128],
                                                                     func=AF.Copy),
                       reads=[PS(bk)], writes=["pay"])
                else:
                    op("act", lambda e, b4=b4, bk=bk: e.activation(
                        out=V_p[:, b4, :, 0:64], in_=ps[bk][:, 0:128].rearrange("p (g d) -> p g d", g=2),
                        func=AF.Copy), reads=[PS(bk)], writes=["V_p"])
                    op("dve", lambda e, b4=b4, bk=bk: e.tensor_copy(out=vstage[:, b4, :], in_=ps[bk][:, 0:128]),
                       reads=[PS(bk), "V_p"], writes=["vstage"])
        dma("sp", vout_d[l, :, :, :], vstage, reads=["vstage"], skey="vout", is_out=True)
        for pi, c0 in enumerate((1408, 1920, 2432)):
            sP, (wv_,) = wpiece([(0, [128, KC, 512], wsrc_rows(W, c0, c0 + 512))])
            for tile in range(NTILE):
                hreg = ("h", tile)
                for c in range(4):
                    bk = c % 2
                    mm_group(ps[bk][:, :], [(wv_[:, k, c * 128:(c + 1) * 128], h1[:, k, tsl(tile)])
                                            for k in range(KC)],
                             reads=[("w", sP), hreg], writes=[PS(bk)])
                    if pi == 1:
                        op("act", lambda e, bk=bk, c=c, tile=tile: e.activation(
                            out=gy[:, c, tsl(tile)], in_=ps[bk][:, :], func=AF.Gelu_apprx_tanh),
                           reads=[PS(bk)], writes=["gy"])
                    elif tile < 2:
                        dst = (pay_xr if pi == 0 else pay_xf)[:, c, tsl(tile)]
                        op("act", lambda e, bk=bk, dst=dst: e.activation(out=dst, in_=ps[bk][:, :], func=AF.Copy),
                           reads=[PS(bk)], writes=["pay"])
                    elif pi == 0:
                        op("act", lambda e, bk=bk, c=c: e.activation(
                            out=xr_p[:, c, :, 2:258], in_=ps[bk][:, :].rearrange("p (s t) -> p s t", s=2),
                            func=AF.Copy), reads=[PS(bk)], writes=["xr_p"])
                    else:
                        op("act", lambda e, bk=bk, c=c: e.activation(out=xfp[:, c, :], in_=ps[bk][:, :],
                                                                     func=AF.Copy),
                           reads=[PS(bk)], writes=["xfp"])

        if STOP == "m1_%d" % l:
            return True
        pool = bld.engs["pool"]
        if "cc" not in bld.dsems:
            bld.dsems["cc"] = [nc.semaphore("cc_sem").__enter__(), 0]
        ccs = bld.dsems["cc"]
        for (pt, gt, c0, c1, pr, gr) in ((payA_t, gatA_t, 0, 6144, "payA_d", "gat_d"),
                                         (payB_t, gatB_t, 6144, 10240, "payB_d", "gatB_d")):
            dma("pool", pt.ap()[:, :], pay_sb[:, c0:c1], reads=["pay"], writes=[pr], skey=pr)
            for ev in bld._deps([pr], [gr]):
                bld.wait(pool, ev)
            ccs[1] += 1
            nc.gpsimd.collective_compute("AllGather", ALU.bypass,
                                         replica_groups=[[0, 1], [2, 3], [4, 5], [6, 7]],
                                         ins=[pt.ap().opt()], outs=[gt.ap().opt()]).then_inc(ccs[0], 1)
            bld._commit(Ev(ccs[0], ccs[1]), [pr], [gr])
        phase_barrier()

        if STOP == "m2_%d" % l:
            return True
        PT = [carve(R5, [128, 5, 512], BF16), carve(R5 + 5 * KB, [128, 5, 512], BF16)]
        K_pad = carve(R5 + 10 * KB, [128, 18 * 128], BF16)
        V_pad = carve(R5 + 15 * KB, [128, 18, 2, 65], BF16)
        K_ext = carve(R5 + 20 * KB, [128, 10 * 128], BF16)
        V_ext = carve(R5 + 23 * KB, [128, 10, 2, 65], BF16)
        o_s, _ = SP_OFF["sel"]
        sel0 = smallp[:, o_s:o_s + 1]
        sel1 = smallp[:, o_s + 1:o_s + 2]
        ctr = [0, 0]
        apend = []
        atoks = [atok, carve(R5 + 26 * KB, [128, 512], BF16)]
        dens = [den, carve(R5 + 27 * KB, [128, 8], F32)]

        for sq_ in range(2):
            kbs = []
            for kb in range(2):
                col = sq_ * 256 + kb * 128
                kbs.append((kT_p[:, col:col + 128], V_p[:, sq_ * 2 + kb, :, :], None, ["kT_p", "V_p"]))
            for qb in range(2):
                q0 = 1024 + sq_ * 256 + qb * 128
                attention(l, qT, q0, kbs, [PT], atoks, dens, attnT, q0, ctr, apend)

        while apend:
            apend.pop(0)()
        if STOP == "m3pa_%d" % l:
            return True
        ABp = carve(R5 + 28 * KB, [128, 4, 256], BF16)
        for g in range(4):
            for sq_ in range(2):
                for nb in range(2):
                    col = sq_ * 256 + nb * 128
                    hb_ = (sq_ * 2 + nb) % 2
                    mm_group(ps[0][:, hb_ * 256:hb_ * 256 + 256],
                             [(xfp[:, g, col:col + 128], dftc[:, :])], reads=["xfp"], writes=[PS(0)])
                    op("act", lambda e, sq_=sq_, nb=nb, hb_=hb_: e.activation(
                        out=ABp[:, sq_ * 2 + nb, :], in_=ps[0][:, hb_ * 256:hb_ * 256 + 256], func=AF.Copy),
                       reads=[PS(0)], writes=["ABp"])
            for sq_ in range(2):
                pairs = []
                for cs in range(2):
                    for nb in range(2):
                        pairs.append((ABp[:, sq_ * 2 + nb, cs * 128:(cs + 1) * 128], dftp[:, nb, cs, :]))
                mm_group(ps[1][:, sq_ * 256:(sq_ + 1) * 256], pairs, reads=["ABp"], writes=[PS(1)])
            op("act", lambda e, g=g: e.activation(out=fourT[:, g, 1024:1536], in_=ps[1][:, :], func=AF.Copy),
               reads=[PS(1)], writes=["fourT"])
        modq = list(range(12)) if l + 1 < DEPTH else []

        def mod_some(n):
            for _ in range(n):
                if modq:
                    emit_mod_piece(l + 1, modq.pop(0))
        lbp = {"xcb": carve(R5 + 24 * KB, [128, 512], BF16), "dg": carve(R5 + 30 * KB, [128, 4, 128], BF16),
               "a": [carve(R5 + 12 * KB, [128, 512], F32)], "s": [carve(R5 + 14 * KB, [128, 512], F32)],
               "b": [carve(R5 + 16 * KB, [128, 512], F32)], "h": [carve(R5 + 18 * KB, [128, 512], F32)],
               "tr": [rt1, carve(69 * KB + 64, [128, TT], F32)], "ti": [rt2, carve(R5 + 22 * KB, [128, TT], F32)],
               "carry": carve(R5 + 25 * KB, [128, 8], F32), "nset": 1}
        accp = carve(R5 + 20 * KB, [128, 512], F32)
        lcnt = [0, 0, 0]
        for c in range(4):
            def fin_p(d, u0, n, h_, hreg, c=c):
                if d == 0:
                    op("dve", lambda e: e.tensor_copy(out=accp[:, 0:512], in_=h_[:, 0:512]), reads=[hreg],
                       writes=["accp"])
                    for sq_ in range(2):
                        col = l * 16 + sq_ * 8 + 0 * 4 + c
                        op("dve", lambda e, col=col, sq_=sq_: e.tensor_copy(
                            out=sout[:, col:col + 1], in_=h_[:, sq_ * 256 + 255: sq_ * 256 + 256]),
                           reads=[hreg], writes=["sout"])
                else:
                    for sq_ in range(2):
                        col = l * 16 + sq_ * 8 + 1 * 4 + c
                        op("dve", lambda e, col=col, sq_=sq_: e.tensor_copy(
                            out=sout[:, col:col + 1], in_=h_[:, sq_ * 256: sq_ * 256 + 1]),
                           reads=[hreg], writes=["sout"])
                    op("dve", lambda e: e.tensor_tensor(out=accp[:, 0:512], in0=accp[:, 0:512], in1=h_[:, 0:512],
                                                        op=ALU.add), reads=[hreg, "accp"], writes=["accp"])
                    op("dve", lambda e: e.tensor_tensor(out=gy[:, c, 1024:1536], in0=accp[:, 0:512],
                                                        in1=gy[:, c, 1024:1536], op=ALU.mult),
                       reads=["accp", "gy"], writes=["gy"])
            def csrc_p(t, j, c=c):
                return (xr_p[:, c, :, j:j + 256], lambda pa: pa.rearrange("p (s t) -> p s t", s=2), ["xr_p"])
            lru_conv(l, c, 512, csrc_p, lbp["xcb"], "xcb", lbp["dg"], "dg", lcnt)
            lru_chunk2(l, c, 512, lbp["xcb"], "xcb", [(0, 512, [(0, 256, 0.0, 0.0), (256, 256, 0.0, 0.0)])],
                       lbp, fin_p, lcnt)
        phase_barrier()
        op("dve", lambda e: e.memset(K_pad, 0.0), writes=["K_pad"])
        op("dve", lambda e: e.memset(V_pad, 1.0), writes=["V_pad"])
        for r_ in range(2):
            dma("sp", K_pad[:, 128 + r_ * 1024: 128 + (r_ + 1) * 1024], gat_d[r_ * 128:(r_ + 1) * 128, 0:1024],
                reads=["gat_d"], writes=["K_pad"], skey="kpad")
            for g in range(2):
                dma("sp", V_pad[:, 1 + r_ * 8: 9 + r_ * 8, g, 0:64],
                    gat_d[r_ * 128:(r_ + 1) * 128, 1024:2048].rearrange("p (b g d) -> p b g d", b=8, g=2)[:, :, g, :],
                    reads=["gat_d"], writes=["V_pad"], skey="vpad")
        op("dve", lambda e: e.tensor_scalar(out=K_ext, in0=K_pad[:, 0:1280], scalar1=sel0, scalar2=None,
                                            op0=ALU.mult), reads=["K_pad"], writes=["K_ext"])
        op("dve", lambda e: e.scalar_tensor_tensor(out=K_ext, in0=K_pad[:, 1024:2304], scalar=sel1, in1=K_ext,
                                                   op0=ALU.mult, op1=ALU.add),
           reads=["K_pad", "K_ext"], writes=["K_ext"])
        op("dve", lambda e: e.tensor_scalar(out=V_ext, in0=V_pad[:, 0:10, :, :], scalar1=sel0, scalar2=None,
                                            op0=ALU.mult), reads=["V_pad"], writes=["V_ext"])
        op("dve", lambda e: e.scalar_tensor_tensor(out=V_ext, in0=V_pad[:, 8:18, :, :], scalar=sel1, in1=V_ext,
                                                   op0=ALU.mult, op1=ALU.add),
           reads=["V_pad", "V_ext"], writes=["V_ext"])
        PT2 = [carve(R5 + 10 * KB, [128, 5, 512], BF16), carve(R5 + 15 * KB, [128, 5, 512], BF16)]
        ctr[1] = 0
        for n in range(8):
            mL = masks[:, 0 if n == 0 else 1, :]
            mR = masks[:, 3 if n == 7 else 2, :]
            kbs = [
                (K_ext[:, n * 128:(n + 1) * 128], V_ext[:, n, :, :], bcast_mid(mL, 4), ["K_ext", "V_ext"]),
                (K_ext[:, (n + 1) * 128:(n + 2) * 128], V_ext[:, n + 1, :, :], None, ["K_ext", "V_ext"]),
                (K_ext[:, (n + 2) * 128:(n + 3) * 128], V_ext[:, n + 2, :, :], bcast_mid(mR, 4),
                 ["K_ext", "V_ext"]),
                (ctxk[:, l, 0:128], ctxv[:, l, 0, :, :], None, []),
                (ctxk[:, l, 128:256], ctxv[:, l, 1, :, :], None, []),
            ]
            attention(l, qT, n * 128, kbs, [PT, PT2], atoks, dens, attnT, n * 128, ctr, apend)
        while apend:
            apend.pop(0)()
        phase_barrier()

        if STOP == "m3sa_%d" % l:
            return True
        lb = {"xcb": carve(R5 + 8 * KB, [128, 2048], BF16), "dg": carve(R5 + 30 * KB, [128, 4, 128], BF16),
              "a": [carve(R5 + 12 * KB, [128, 512], F32)], "s": [carve(R5 + 14 * KB, [128, 512], F32)],
              "b": [carve(R5 + 16 * KB, [128, 512], F32)], "h": [carve(R5 + 18 * KB, [128, 512], F32)],
              "tr": [rt1, carve(69 * KB + 64, [128, TT], F32)], "ti": [rt2, carve(44 * KB + 256, [128, TT], F32)],
              "carry": den, "nset": 1}
        xrf = carve(36 * KB, [128, 2052], BF16)
        acc = carve(40 * KB + 256, [128, 1024], F32)
        o_h0, _ = SP_OFF["h0"]
        phase_barrier()
        lb["a"] = [carve(R5 + 12 * KB, [128, 2048], F32)]
        lb["s"] = [carve(R5 + 20 * KB, [128, 2048], F32)]
        lb["b"] = [carve(48 * KB, [128, 2048], F32)]
        lb["h"] = lb["s"]
        lb["h_alias_s"] = True
        xcbs = [lb["xcb"], carve(56 * KB, [128, 2048], BF16)]
        dga = carve(R5 + 28 * KB, [128, 4, 4, 128], BF16)
        o_w_, _ = SP_OFF["convw"]
        for c_ in range(4):
            for j_ in range(4):
                cwj = smallp[:, o_w_ + l * 16 + j_ * 4 + c_: o_w_ + l * 16 + j_ * 4 + c_ + 1]
                op("dve", lambda e, c_=c_, j_=j_, cwj=cwj: e.tensor_scalar(
                    out=dga[:, c_, j_, :], in0=ident[:], scalar1=cwj, scalar2=None, op0=ALU.mult),
                   writes=["dga"])
        xrfs = [xrf, carve(R5, [128, 2052], BF16)]
        for xb_ in xrfs:
            op("dve", lambda e, xb_=xb_: e.memset(xb_[:, 0:2], 0.0), writes=[("xrf", id(xb_))])
            op("dve", lambda e, xb_=xb_: e.memset(xb_[:, 2050:2052], 0.0), writes=[("xrf", id(xb_))])
        def load_xrf(c):
            xrf_c = xrfs[c % 2]
            xreg = ("xrf", id(xrf_c))
            for r_ in range(2):
                dma("sp", xrf_c[:, 2 + r_ * 1024: 2 + (r_ + 1) * 1024],
                    gat_d[r_ * 128:(r_ + 1) * 128, 2048 + c * 1024: 2048 + (c + 1) * 1024],
                    reads=["gat_d"], writes=[xreg], skey=("xrf", c % 2))

        def conv_s(c):
            xrf_c = xrfs[c % 2]
            xreg = ("xrf", id(xrf_c))

            def csrc_s(t, j):
                return (xrf_c[:, t * TT + j: t * TT + j + TT], lambda pa: pa, [xreg])
            lru_conv(l, c, 2048, csrc_s, xcbs[c % 2], ("xcb", c % 2), dga[:, c, :, :], "dga", lcnt, make_dg=False)

        load_xrf(0)
        load_xrf(1)
        conv_s(0)
        for c in range(4):
            h0f = smallp[:, o_h0 + l * 8 + 0 * 4 + c: o_h0 + l * 8 + 0 * 4 + c + 1]
            h0b = smallp[:, o_h0 + l * 8 + 1 * 4 + c: o_h0 + l * 8 + 1 * 4 + c + 1]

            def fin_s(d, u0, n, h_, hreg, c=c):
                if d == 0:
                    op("dve", lambda e: e.tensor_scalar(out=acc, in0=h_[:, 0:1024], scalar1=sel0, scalar2=None,
                                                        op0=ALU.mult), reads=[hreg], writes=["acc"])
                else:
                    op("dve", lambda e: e.scalar_tensor_tensor(out=acc, in0=h_[:, 0:1024], scalar=sel0, in1=acc,
                                                               op0=ALU.mult, op1=ALU.add),
                       reads=[hreg, "acc"], writes=["acc"])
                op("dve", lambda e: e.scalar_tensor_tensor(out=acc, in0=h_[:, 1024:2048], scalar=sel1, in1=acc,
                                                           op0=ALU.mult, op1=ALU.add),
                   reads=[hreg, "acc"], writes=["acc"])
                if d == 1:
                    op("dve", lambda e: e.tensor_tensor(out=gy[:, c, 0:1024], in0=acc, in1=gy[:, c, 0:1024],
                                                        op=ALU.mult), reads=["acc", "gy"], writes=["gy"])
            units = [(0, 2048, [(0, 2048, h0f, h0b)])]

            def mid(c=c):
                if c + 1 < 4:
                    conv_s(c + 1)
                if c + 2 < 4:
                    pass
            lru_chunk2(l, c, 2048, xcbs[c % 2], ("xcb", c % 2), units, lb, fin_s, lcnt, mid_hook=mid)
            if c + 2 < 4:
                load_xrf(c + 2)
            mod_some(3)
        mod_some(12)
        phase_barrier()

        if STOP == "lru_%d" % l:
            return True
        ABall = carve(R5, [128, 16, 4, 256], BF16)
        xfg = carve(36 * KB, [128, 2048], BF16)
        tab = [carve(40 * KB, [128, 4, 512], BF16), carve(44 * KB, [128, 4, 512], BF16)]
        xfa = carve(48 * KB, [128, 4, 2048], BF16)
        for g in range(4):
            for r_ in range(2):
                dma("sp", xfa[:, g, r_ * 1024:(r_ + 1) * 1024],
                    gatB_d[r_ * 128:(r_ + 1) * 128, g * 1024: (g + 1) * 1024],
                    reads=["gatB_d"], writes=[("xfa", g)], skey=("xfa", g))
        for g in range(4):
            for nb in range(16):
                eng_ = "act" if nb % 2 == 0 else "dve"
                bk = nb % 2
                hb_ = (nb // 2) % 2
                mm_group(ps[bk][:, hb_ * 256:(hb_ + 1) * 256],
                         [(xfa[:, g, nb * 128:(nb + 1) * 128], dftc[:, :])], reads=[("xfa", g)], writes=[PS(bk)])
                if eng_ == "act":
                    op("act", lambda e, nb=nb, hb_=hb_, g=g, bk=bk: e.activation(
                        out=ABall[:, nb, g, :], in_=ps[bk][:, hb_ * 256:(hb_ + 1) * 256], func=AF.Copy),
                       reads=[PS(bk)], writes=["AB"])
                else:
                    op("dve", lambda e, nb=nb, hb_=hb_, g=g, bk=bk: e.tensor_copy(
                        out=ABall[:, nb, g, :], in_=ps[bk][:, hb_ * 256:(hb_ + 1) * 256]),
                       reads=[PS(bk)], writes=["AB"])
        tcnt = 0
        for kt in range(2):
            for cs in range(2):
                for nq in range(4):
                    tb = tab[tcnt % 2]
                    treg = ("tab", tcnt % 2)
                    tcnt += 1
                    dma("sp", tb, dfts_d[kt * 2 + cs, :, nq * 4:(nq + 1) * 4, :], writes=[treg], skey=treg)
                    first = (cs == 0 and nq == 0)
                    lastp = (cs == 1 and nq == 3)
                    for g in range(4):
                        pairs = [(ABall[:, nq * 4 + j, g, cs * 128:(cs + 1) * 128], tb[:, j, :]) for j in range(4)]
                        mm_group(ps[2 + g][:, :], pairs, reads=[treg, "AB"],
                                 writes=[PS(2 + g)] if (first or lastp) else [],
                                 first_start=first, last_stop=lastp)
            for g in range(4):
                op("act", lambda e, g=g, kt=kt: e.activation(out=fourT[:, g, kt * 512:(kt + 1) * 512],
                                                            in_=ps[2 + g][:, :], func=AF.Copy),
                   reads=[PS(2 + g)], writes=["fourT"])
        phase_barrier()

        if STOP == "four_%d" % l:
            return True
        h4 = carve(R5, [128, KC, NTOK], BF16)
        mgb = carve(36 * KB, [128, KC, NTOK], BF16)
        NT4 = norm_temps(60 * KB)
        tgs = carve(R5 + 24 * KB, [128, 3, TT], BF16)
        mt1 = carve(R5 + 27 * KB, [128, TT], F32)
        mt2 = carve(R5 + 29 * KB, [128, TT], F32)
        for tile in range(NTILE):
            modnorm(l, 1, tile, h4[:, :, tsl(tile)], NT4)
        branches = (attnT, gy, fourT)
        for m in range(KC):
            sG, (wg_,) = wpiece([(0, [128, KC, 384], w_gate[l, m, :, :, :])])
            sB, (wb_,) = wpiece([(0, [128, 12, 128], w_out[l, m, :, :, :])])
            for tile in range(NTILE):
                hreg = ("h", tile)
                for b_ in range(3):
                    mm_group(ps[b_][:, :], [(wg_[:, k, b_ * 128:(b_ + 1) * 128], h4[:, k, tsl(tile)])
                                            for k in range(KC)],
                             reads=[("w", sG), hreg], writes=[PS(b_)])
                    op("act", lambda e, b_=b_, m=m: e.activation(
                        out=tgs[:, b_, :], in_=ps[b_][:, :], func=AF.Tanh,
                        bias=hbg[:, l * 24 + b_ * 8 + m: l * 24 + b_ * 8 + m + 1], scale=0.5),
                       reads=[PS(b_)], writes=[("tgs", b_)])
                for b_ in range(3):
                    mm_group(ps[3 + b_][:, :], [(wb_[:, b_ * 4 + k, :], branches[b_][:, k, tsl(tile)])
                                                for k in range(4)],
                             reads=[("w", sB), "attnT", "gy", "fourT"], writes=[PS(3 + b_)])
                op("dve", lambda e: e.scalar_tensor_tensor(out=mt1, in0=tgs[:, 0, :], scalar=1.0,
                                                           in1=ps[3][:, :], op0=ALU.add, op1=ALU.mult),
                   reads=[("tgs", 0), PS(3)], writes=["mt1"])
                op("dve", lambda e: e.scalar_tensor_tensor(out=mt2, in0=tgs[:, 1, :], scalar=1.0,
                                                           in1=ps[4][:, :], op0=ALU.add, op1=ALU.mult),
                   reads=[("tgs", 1), PS(4)], writes=["mt2"])
                op("dve", lambda e: e.tensor_tensor(out=mt1, in0=mt1, in1=mt2, op=ALU.add),
                   reads=["mt1", "mt2"], writes=["mt1"])
                op("dve", lambda e: e.scalar_tensor_tensor(out=mt2, in0=tgs[:, 2, :], scalar=1.0,
                                                           in1=ps[5][:, :], op0=ALU.add, op1=ALU.mult),
                   reads=[("tgs", 2), PS(5)], writes=["mt2"])
                op("dve", lambda e, m=m, tile=tile: e.tensor_tensor(out=mgb[:, m, tsl(tile)], in0=mt1, in1=mt2,
                                                                    op=ALU.add),
                   reads=["mt1", "mt2"], writes=[("mgb", tile)])
        wos = []
        for hp in range(2):
            sO, (wo_,) = wpiece([(0, [128, KC, 512], wsrc_rows(w_o[l], hp * 512, (hp + 1) * 512))])
            wos.append((sO, wo_))
        hF = carve(0, [128, KC, NTOK], BF16)
        for tile in range(NTILE):
            ci = cidx(tile)
            for hp in range(2):
                sO, wo_ = wos[hp]
                for mm_ in range(4):
                    m = hp * 4 + mm_
                    bk = mm_ % 2
                    mm_group(ps[bk][:, :], [(wo_[:, k, mm_ * 128:(mm_ + 1) * 128], mgb[:, k, tsl(tile)])
                                            for k in range(KC)],
                             reads=[("w", sO), ("mgb", tile)], writes=[PS(bk)])
                    op("dve", lambda e, m=m, bk=bk, tile=tile, ci=ci: e.scalar_tensor_tensor(
                        out=x[:, m, tsl(tile)], in0=ps[bk][:, :], scalar=modG[:, l, 1, m, ci:ci + 1],
                        in1=x[:, m, tsl(tile)], op0=ALU.mult, op1=ALU.add),
                       reads=[PS(bk), ("x", tile)], writes=[("x", tile)])
            modnorm(l, 2, tile, hF[:, :, tsl(tile)], NT4)
        phase_barrier()


    fin_state = {"done": False}

    def next_norm_hook(l):
        NTn = norm_temps(57 * KB)
        hN = carve(0, [128, KC, NTOK], BF16)
        if l + 1 < DEPTH:
            return lambda tile: modnorm(l + 1, 0, tile, hN[:, :, tsl(tile)], NTn)
        ybs = [carve(67 * KB, [128, KC, TT], F32), carve(87 * KB, [128, KC, TT], F32),
               carve(0, [128, KC, TT], F32)]

        def fin(tile):
            yb = ybs[tile]
            modnorm(0, 0, tile, yb, NTn, A_ap=spv("fnw"))
            dma("sp", y_d[:, :, tsl(tile)], yb, reads=[("h", tile)], writes=[("yout", tile)],
                skey=("y", tile), is_out=True)
            fin_state["done"] = True
        return fin

    def run_all():
        for l in range(DEPTH):
            phase_barrier()
            NTm = norm_temps(57 * KB)
            hM = carve(0, [128, KC, NTOK], BF16)
            ffn(l, 0, ffn_w[1], hook=mod0_hook if l == 0 else None, prenormed=(l > 0 and not STOP),
                after_tile=None if STOP else (lambda tile, l=l: modnorm(l, 1, tile, hM[:, :, tsl(tile)], NTm)))
            while l == 0 and modq0:
                mod0_hook()
            if STOP == "ffn1_%d" % l:
                return
            if mixer(l, m1_prenormed=not STOP):
                return
            if STOP == "mix_%d" % l:
                return
            ffn(l, 2, ffn_w[2], prenormed=True, after_tile=None if STOP else next_norm_hook(l))
            if STOP == "ffn2_%d" % l:
                return

    if STOP != "pre":
        run_all()
    phase_barrier()
    if STOP:
        dbg_d = dout("dbg", [128, ARENA_ELEMS], BF16)
        dma("sp", dbg_d[:, :], arena[:], skey="dbgd", is_out=True)
        dma("sp", y_d[:, :, :], x[:], reads=[("x", 0), ("x", 1), ("x", 2)], skey="ydbg", is_out=True)
    else:
        assert fin_state["done"]
        ybs = []
    dma("sp", sout_d[:, :], sout[:], reads=["sout"], skey="soutd", is_out=True)
    sp = bld.engs["sp"]
    for ev in bld.out_evs:
        bld.wait(sp, ev)
    for en in ("pe", "act", "dve"):
        e_ = bld.engs[en]
        if e_.cnt:
            bld.wait(sp, Ev(e_.sem, e_.cnt))
    return nc


def _fm(v):
    v = np.asarray(v, np.float32)
    lead = v.shape[:-1]
    n = v.shape[-1] // 128
    v = v.reshape(lead + (n, 128))
    return np.moveaxis(v, -1, 0)


_NC_CACHE = {}


def kernel(**inp):
    inp = {k: np.asarray(v) for k, v in inp.items()}
    f32 = np.float32
    ident = np.eye(128, dtype=f32).astype(BF)
    ones = np.ones((128, 128), f32).astype(BF)
    jl = np.arange(128)[:, None]
    il = np.arange(128)[None, :]
    band_l = np.where(jl >= il, 1.0, 0.0).astype(f32)
    band_r = np.where(jl <= il, 1.0, 0.0).astype(f32)
    allm = np.zeros((128, 128), f32)
    cj = np.arange(128)
    ang = 2.0 * np.pi * np.outer(cj, cj) / 128.0
    dftc = np.concatenate([np.cos(ang), np.sin(ang)], axis=1).astype(f32).astype(BF)
    n256 = np.arange(256)
    a256 = 2.0 * np.pi * np.outer(n256, n256) / 256.0
    nrm_p = 1.0 / np.sqrt(256.0 * 128.0)
    tp = np.stack([np.cos(a256) * nrm_p, -np.sin(a256) * nrm_p], axis=1)
    dftp = tp.reshape(2, 128, 2, 256).transpose(1, 0, 2, 3).astype(f32).astype(BF)
    nrm_s = 1.0 / np.sqrt(2048.0 * 128.0)
    n2048 = np.arange(2048, dtype=np.float64)

    def head_cols(h):
        return list(range(h * 64, (h + 1) * 64))

    def swap_d(cols):
        out = []
        for a in range(2):
            out += cols[a * 32 + 16: a * 32 + 32] + cols[a * 32: a * 32 + 16]
        return out
    qcols, qscols = [], []
    for j in range(4):
        for h in (j, 4 + j):
            qcols += head_cols(h)
            qscols += swap_d(head_cols(h))
    kcols = list(range(512, 640))
    kscols = swap_d(list(range(512, 576))) + swap_d(list(range(576, 640)))
    perm = qcols + qscols + kcols + kscols + list(range(640, 768)) + list(range(768, 2304))
    assert len(perm) == NCOLIN
    w_in_p = np.ascontiguousarray(inp["w_in"][:, :, perm])

    lruw = np.zeros((128, DEPTH * 16, 128), f32)
    for l in range(DEPTH):
        for d in range(2):
            for gi, wname in enumerate(("lru_wa", "lru_wi")):
                w = inp[wname][l, d]
                for c in range(4):
                    for bb in range(2):
                        lruw[bb * 64:(bb + 1) * 64, l * 16 + d * 8 + gi * 4 + c, bb * 64:(bb + 1) * 64] = w[c * 2 + bb]

    wg = inp["w_branch_gate"].reshape(DEPTH, KC, 128, 3, 8, 128)
    w_gate_p = np.ascontiguousarray(wg.transpose(0, 4, 2, 1, 3, 5)).reshape(DEPTH, 8, 128, KC, 384)
    wo3 = np.stack([inp["w_attn_out"], inp["w_lru_out"], inp["w_fourier_out"]], axis=1)
    wo3 = wo3.reshape(DEPTH, 3, 4, 128, 8, 128)
    w_out_p = np.ascontiguousarray(wo3.transpose(0, 4, 3, 1, 2, 5)).reshape(DEPTH, 8, 128, 12, 128)
    shared = {
        "ident": ident, "ones": ones, "dftc": dftc, "dftp": dftp, "lruw": lruw,
        "w_ada": inp["w_ada"], "ffn1_wg": inp["ffn1_wg"], "ffn1_wu": inp["ffn1_wu"], "ffn1_wd": inp["ffn1_wd"],
        "ffn2_wg": inp["ffn2_wg"], "ffn2_wu": inp["ffn2_wu"], "ffn2_wd": inp["ffn2_wd"],
        "w_in": w_in_p, "w_gate": w_gate_p, "w_out": w_out_p, "w_o": inp["w_o"],
    }
    inv = 10000.0 ** (-np.arange(0, 32, 2, dtype=np.float64) / 32.0)

    in_maps = []
    for i in range(8):
        s, r = i // 2, i % 2
        xs = inp["x_sample"][s, r * 1024:(r + 1) * 1024]
        xp = inp["x_prompt"][2 * i: 2 * i + 2].reshape(512, D)
        xall = np.concatenate([xs, xp], axis=0)
        xin = np.ascontiguousarray(xall.reshape(NTOK, KC, 128).transpose(2, 1, 0))
        sp_ = np.zeros((128, NS), f32)

        def put(name, arr):
            o, w = SP_OFF[name]
            arr = np.asarray(arr, f32).reshape(128, -1)
            sp_[:, o:o + arr.shape[1]] = arr
        cond = np.stack([_fm(inp["c"][s]), _fm(inp["c_ctx"])], axis=-1)
        put("cond", cond)
        put("bada", _fm(inp["b_ada"]).reshape(128, DEPTH * 72))
        put("normw", _fm(inp["norm_w"]).reshape(128, DEPTH * 24))
        put("fnw", _fm(inp["final_norm_w"]))
        put("bgate", _fm(inp["b_branch_gate"]).reshape(128, DEPTH * 24))
        put("convw", _fm(inp["conv_w"]).reshape(128, DEPTH * 16))
        put("convb", _fm(inp["conv_b"]).reshape(128, DEPTH * 4))
        put("lba", _fm(inp["lru_ba"]).reshape(128, DEPTH * 8))
        put("lbi", _fm(inp["lru_bi"]).reshape(128, DEPTH * 8))
        put("lam", _fm(inp["lru_lambda"]).reshape(128, DEPTH * 8))
        put("sink", np.broadcast_to(inp["attn_sink"].reshape(1, DEPTH * 8), (128, DEPTH * 8)))
        put("h0", _fm(inp["state_lru"][s]).reshape(128, DEPTH * 8))
        selv = np.zeros((128, 2), f32)
        selv[:, r] = 1.0
        put("sel", selv)
        masks = np.stack([allm if r == 0 else band_l, band_l, band_r, allm if r == 1 else band_r], axis=1)
        tg_ = r * 1024 + np.arange(1024)
        pos = np.stack([tg_ // 64, tg_ % 64], axis=0).astype(np.float64)
        C = np.zeros((64, 1024))
        S = np.zeros((64, 1024))
        for a in range(2):
            angl = inv[:, None] * pos[a][None, :]
            for half in range(2):
                rows = slice(a * 32 + half * 16, a * 32 + half * 16 + 16)
                C[rows] = np.cos(angl)
                S[rows] = np.sin(angl) * (-1.0 if half == 0 else 1.0)
        rope = np.stack([np.concatenate([C, C], 0), np.concatenate([S, S], 0)], axis=1).astype(f32).astype(BF)
        ctxk = np.ascontiguousarray(inp["cache_k"][s].reshape(DEPTH, 256, 128).transpose(2, 0, 1))
        ctxv = np.ascontiguousarray(
            inp["cache_v"][s].reshape(DEPTH, 2, 128, 128).transpose(2, 0, 1, 3))
        kk = (r * 1024 + np.arange(1024, dtype=np.float64))
        angs = 2.0 * np.pi * np.outer(n2048, kk) / 2048.0
        angs = np.mod(angs, 2.0 * np.pi)
        tabs = np.stack([np.cos(angs) * nrm_s, -np.sin(angs) * nrm_s], axis=0)
        tabs = tabs.reshape(2, 16, 128, 2, 512).transpose(3, 0, 2, 1, 4)
        dfts = np.ascontiguousarray(tabs.reshape(4, 128, 16, 512)).astype(f32).astype(BF)
        m = dict(shared)
        m.update({"xin": xin, "smallp": sp_, "masks": np.ascontiguousarray(masks).astype(BF), "rope": rope,
                  "ctxk": ctxk, "ctxv": ctxv, "dfts": dfts})
        in_maps.append(m)

    if "nc" not in _NC_CACHE:
        _NC_CACHE["nc"] = build()
    nc = _NC_CACHE["nc"]
    res = run_bass_kernel_spmd(nc, in_maps, core_ids=list(range(8)))
    outs = res.results
    _NC_CACHE["last"] = outs

    y_prompt = np.zeros((16, 256, D), f32)
    y_sample = np.zeros((4, 2048, D), f32)
    nk = np.zeros((16, DEPTH, 256, 2, 64), f32)
    nv = np.zeros((16, DEPTH, 256, 2, 64), f32)
    ns = np.zeros((16, DEPTH, 2, 512), f32)
    for i in range(8):
        s, r = i // 2, i % 2
        o = outs[i]
        y = np.asarray(o["y"], f32)
        yt = y.transpose(2, 1, 0).reshape(NTOK, D)
        y_sample[s, r * 1024:(r + 1) * 1024] = yt[:1024]
        y_prompt[2 * i: 2 * i + 2] = yt[1024:].reshape(2, 256, D)
        ko = np.asarray(o["kout"], f32)
        vo = np.asarray(o["vout"], f32)
        so = np.asarray(o["sout"], f32)
        for l in range(DEPTH):
            kt_ = ko[l].T.reshape(2, 256, 2, 64)
            vt_ = vo[l].transpose(1, 0, 2).reshape(512, 128).reshape(2, 256, 2, 64)
            for q in range(2):
                nk[2 * i + q, l] = kt_[q]
                nv[2 * i + q, l] = vt_[q]
                for d in range(2):
                    cols = so[:, l * 16 + q * 8 + d * 4: l * 16 + q * 8 + d * 4 + 4]
                    ns[2 * i + q, l, d] = cols.T.reshape(512)
    return (y_prompt, y_sample, nk, nv, ns)
```
